# Optimizing a Trainium2 kernel written in Bass

```python
import math
import jax, jax.numpy as jnp
from jax import lax
import numpy as np

D_MODEL = 1024
BATCH = 16
SEQ = 2048
DEPTH = 2
DEC_BATCH = 8
DEC_SEQ = 8192
PAST_LEN = 128

N_HEADS = 16
N_KV_HEADS = 4
HEAD_DIM = D_MODEL // N_HEADS
Q_WIDTH = N_HEADS * HEAD_DIM
KV_WIDTH = N_KV_HEADS * HEAD_DIM
QKV_WIDTH = Q_WIDTH + 2 * KV_WIDTH
A_HALF_WINDOW = 128
A_BLOCK = 128
DILATED_GROUPS = ((128, 1), (512, 4), (2048, 16))
B_BLOCK = 64
NUM_BUCKETS = 32
MAX_DISTANCE = 1024
D_FF = ((8 * D_MODEL // 3 + 255) // 256) * 256
N_A_LAYERS = (DEPTH + 1) // 2
N_B_LAYERS = DEPTH // 2
EPS = 1e-6
NEG = -1e30

kernel_name = "hybrid_windowed_dilated_encoder"


def rms_norm(x, g):
    xf = x.astype(jnp.float32)
    y = xf * lax.rsqrt(jnp.mean(xf * xf, axis=-1, keepdims=True) + EPS)
    return (y * g.astype(jnp.float32)).astype(x.dtype)


def t5_buckets(rel):
    nb = NUM_BUCKETS // 2
    max_exact = nb // 2
    n = np.abs(rel)
    large = max_exact + (np.log(np.maximum(n, 1) / max_exact)
                         / math.log(MAX_DISTANCE / max_exact) * (nb - max_exact)).astype(np.int32)
    large = np.minimum(large, nb - 1)
    return ((rel > 0).astype(np.int32) * nb + np.where(n < max_exact, n, large)).astype(np.int32)


def relative_bias(rel_table, blk, dil):
    qi = np.arange(blk)[:, None]
    kj = np.arange(3 * blk)[None, :]
    buckets = t5_buckets(dil * (kj - blk - qi))
    return jnp.take(rel_table, jnp.asarray(buckets), axis=0).transpose(2, 0, 1).astype(jnp.float32)


def banded_attention(q, k, v, bias, blk, half_window, sink):
    N, L, H, Dh = q.shape
    G = k.shape[2]
    R = H // G
    nb = -(-L // blk)
    pad = nb * blk - L
    qp = jnp.pad(q, ((0, 0), (0, pad), (0, 0), (0, 0)))
    kp = jnp.pad(k, ((0, 0), (blk, blk + pad), (0, 0), (0, 0)))
    vp = jnp.pad(v, ((0, 0), (blk, blk + pad), (0, 0), (0, 0)))
    rel = np.arange(3 * blk)[None, :] - blk - np.arange(blk)[:, None]
    band = jnp.asarray(np.abs(rel) <= half_window)
    bias_g = bias.reshape(G, R, blk, 3 * blk)
    scale = Dh ** -0.5

    def one_block(n):
        start = n * blk
        qb = lax.dynamic_slice_in_dim(qp, start, blk, axis=1).reshape(N, blk, G, R, Dh).astype(jnp.float32)
        kb = lax.dynamic_slice_in_dim(kp, start, 3 * blk, axis=1).astype(jnp.float32)
        vb = lax.dynamic_slice_in_dim(vp, start, 3 * blk, axis=1).astype(jnp.float32)
        kpos = start - blk + jnp.arange(3 * blk)
        valid = jnp.logical_and(band, ((kpos >= 0) & (kpos < L))[None, :])
        s = jnp.einsum('nqgrd,nkgd->ngrqk', qb, kb) * scale + bias_g
        s = jnp.where(valid, s, NEG)
        lse = jax.nn.logsumexp(s, axis=-1)
        if sink is not None:
            lse = jnp.logaddexp(lse, sink.astype(jnp.float32).reshape(G, R)[None, :, :, None])
        p = jnp.exp(s - lse[..., None])
        o = jnp.einsum('ngrqk,nkgd->nqgrd', p, vb).reshape(N, blk, H, Dh)
        return o, lse.transpose(0, 3, 1, 2).reshape(N, blk, H)

    o, lse = lax.map(one_block, jnp.arange(nb))
    o = jnp.moveaxis(o, 0, 1).reshape(N, nb * blk, H, Dh)[:, :L]
    lse = jnp.moveaxis(lse, 0, 1).reshape(N, nb * blk, H)[:, :L]
    return o, lse


def split_heads(qkv_g, Bn, S):
    q = qkv_g[..., :Q_WIDTH].reshape(Bn, S, N_HEADS, HEAD_DIM)
    k = qkv_g[..., Q_WIDTH:Q_WIDTH + KV_WIDTH].reshape(Bn, S, N_KV_HEADS, HEAD_DIM)
    v = qkv_g[..., Q_WIDTH + KV_WIDTH:].reshape(Bn, S, N_KV_HEADS, HEAD_DIM)
    return q, k, v


def mixer_windowed(h, w_qkv, q_gain, k_gain, sink, w_o, rel_table):
    Bn, S, _ = h.shape
    q, k, v = split_heads(h @ w_qkv, Bn, S)
    q = rms_norm(q, q_gain)
    k = rms_norm(k, k_gain)
    bias = relative_bias(rel_table, A_BLOCK, 1)
    o, _ = banded_attention(q, k, v, bias, A_BLOCK, A_HALF_WINDOW, sink)
    return o.astype(h.dtype).reshape(Bn, S, Q_WIDTH) @ w_o


def mixer_dilated(h, w_qkv, q_gain, k_gain, w_o, rel_table):
    Bn, S, _ = h.shape
    qkv = h @ w_qkv
    outs, lses = [], []
    for gi, (window, dil) in enumerate(DILATED_GROUPS):
        q, k, v = split_heads(qkv[..., gi * QKV_WIDTH:(gi + 1) * QKV_WIDTH], Bn, S)
        q = rms_norm(q, q_gain[gi])
        k = rms_norm(k, k_gain[gi])
        Ls = S // dil
        fold = lambda t: t.reshape(Bn, Ls, dil, t.shape[2], HEAD_DIM).swapaxes(1, 2).reshape(Bn * dil, Ls, t.shape[2], HEAD_DIM)
        bias = relative_bias(rel_table, B_BLOCK, dil)
        o, lse = banded_attention(fold(q), fold(k), fold(v), bias, B_BLOCK, window // (2 * dil), None)
        outs.append(o.reshape(Bn, dil, Ls, N_HEADS, HEAD_DIM).swapaxes(1, 2).reshape(Bn, S, N_HEADS, HEAD_DIM))
        lses.append(lse.reshape(Bn, dil, Ls, N_HEADS).swapaxes(1, 2).reshape(Bn, S, N_HEADS))
    wts = jax.nn.softmax(jnp.stack(lses), axis=0)
    o = jnp.einsum('gbsh,gbshd->bshd', wts, jnp.stack(outs))
    return o.astype(h.dtype).reshape(Bn, S, Q_WIDTH) @ w_o


def swiglu(h, w_gate_up, w_down):
    gu = h @ w_gate_up
    return (jax.nn.silu(gu[..., :D_FF]) * gu[..., D_FF:]) @ w_down


def trunk(x, rel_table, norm_attn, norm_ffn, a_w_qkv, a_q_gain, a_k_gain, a_sink, a_w_o,
          b_w_qkv, b_q_gain, b_k_gain, b_w_o, ffn_w_gate_up, ffn_w_down):
    for i in range(DEPTH):
        h = rms_norm(x, norm_attn[i])
        j = i // 2
        if i % 2 == 0:
            x = x + mixer_windowed(h, a_w_qkv[j], a_q_gain[j], a_k_gain[j], a_sink[j], a_w_o[j], rel_table)
        else:
            x = x + mixer_dilated(h, b_w_qkv[j], b_q_gain[j], b_k_gain[j], b_w_o[j], rel_table)
        x = x + swiglu(rms_norm(x, norm_ffn[i]), ffn_w_gate_up[i], ffn_w_down[i])
    return x


def setup_inputs(seed: int = 0) -> dict:
    key = jax.random.key(seed)
    ks = jax.random.split(key, 16)
    f32 = jnp.float32
    nrm = lambda k, shape, s: jax.random.normal(k, shape, f32) * s
    n_grp = len(DILATED_GROUPS)
    return {
        "x_prompt": nrm(ks[0], (BATCH, SEQ, D_MODEL), 1.0),
        "x_sample": nrm(ks[1], (DEC_BATCH, DEC_SEQ, D_MODEL), 1.0),
        "rel_table": nrm(ks[2], (NUM_BUCKETS, N_HEADS), 0.5),
        "norm_attn": 1.0 + nrm(ks[3], (DEPTH, D_MODEL), 0.02),
        "norm_ffn": 1.0 + nrm(ks[4], (DEPTH, D_MODEL), 0.02),
        "a_w_qkv": nrm(ks[5], (N_A_LAYERS, D_MODEL, QKV_WIDTH), D_MODEL ** -0.5),
        "a_q_gain": 1.0 + nrm(ks[6], (N_A_LAYERS, HEAD_DIM), 0.02),
        "a_k_gain": 1.0 + nrm(ks[7], (N_A_LAYERS, HEAD_DIM), 0.02),
        "a_sink": nrm(ks[8], (N_A_LAYERS, N_HEADS), 1.0),
        "a_w_o": nrm(ks[9], (N_A_LAYERS, Q_WIDTH, D_MODEL), Q_WIDTH ** -0.5),
        "b_w_qkv": nrm(ks[10], (N_B_LAYERS, D_MODEL, n_grp * QKV_WIDTH), D_MODEL ** -0.5),
        "b_q_gain": 1.0 + nrm(ks[11], (N_B_LAYERS, n_grp, HEAD_DIM), 0.02),
        "b_k_gain": 1.0 + nrm(ks[12], (N_B_LAYERS, n_grp, HEAD_DIM), 0.02),
        "b_w_o": nrm(ks[13], (N_B_LAYERS, Q_WIDTH, D_MODEL), Q_WIDTH ** -0.5),
        "ffn_w_gate_up": nrm(ks[14], (DEPTH, D_MODEL, 2 * D_FF), D_MODEL ** -0.5),
        "ffn_w_down": nrm(ks[15], (DEPTH, D_FF, D_MODEL), D_FF ** -0.5),
    }


def reference(x_prompt, x_sample, rel_table, norm_attn, norm_ffn, a_w_qkv, a_q_gain, a_k_gain,
              a_sink, a_w_o, b_w_qkv, b_q_gain, b_k_gain, b_w_o, ffn_w_gate_up, ffn_w_down):
    y_prompt = trunk(x_prompt, rel_table, norm_attn, norm_ffn, a_w_qkv, a_q_gain, a_k_gain, a_sink, a_w_o,
                     b_w_qkv, b_q_gain, b_k_gain, b_w_o, ffn_w_gate_up, ffn_w_down)
    y_sample = trunk(x_sample, rel_table, norm_attn, norm_ffn, a_w_qkv, a_q_gain, a_k_gain, a_sink, a_w_o,
                     b_w_qkv, b_q_gain, b_k_gain, b_w_o, ffn_w_gate_up, ffn_w_down)
    return (y_prompt, y_sample)
```

```python
import math
import os
from contextlib import ExitStack

import numpy as np
import concourse.bass as bass
import concourse.mybir as mybir
from concourse.bass_utils import run_bass_kernel_spmd

F32 = mybir.dt.float32
BF16 = mybir.dt.bfloat16
AF = mybir.ActivationFunctionType
ALU = mybir.AluOpType

D = 1024
DFF = 2816
NCH = 22
EPS = 1e-6
NEGB = -30000.0
SPAN = 2048
ENGS = ["tensor", "vector", "scalar", "gpsimd", "sync"]

FULL_SEGS = [(0, 8192), (8192, 2048), (10240, 2048)]


class Sched:
    def __init__(self):
        self.phase = 0
        self._reset()

    def _reset(self):
        self.ops = {e: [] for e in ENGS}
        self.lastw = {}
        self.readers = {}
        if not hasattr(self, "cnt"):
            self.cnt = {e: 0 for e in ENGS}
            self.waited = {e: {} for e in ENGS}
            self.dmacnt = {}
            self.keys = []
            for e in ENGS:
                self._key(("E", e))

    def _key(self, k):
        if k not in self.keys:
            self.keys.append(k)
        return k

    def _collect(self, eng, reads, writes):
        deps = []
        for r in reads:
            ev = self.lastw.get(r)
            if ev is not None:
                deps.append((ev, True))
        for w in writes:
            ev = self.lastw.get(w)
            if ev is not None:
                deps.append((ev, False))
            for ev in self.readers.get(w, ()):
                deps.append((ev, False))
        waits = {}
        for (key, val, src), is_raw in deps:
            if src == eng:
                if eng == "tensor" or not is_raw:
                    continue
            if src == "dma":
                val = self.dmacnt[key]
            if self.waited[eng].get(key, 0) >= val:
                continue
            if waits.get(key, 0) < val:
                waits[key] = val
        for k, v in waits.items():
            self.waited[eng][k] = v
        return list(waits.items())

    def _record(self, ev, reads, writes):
        for r in reads:
            self.readers.setdefault(r, []).append(ev)
        for w in writes:
            self.lastw[w] = ev
            self.readers[w] = []

    def op(self, eng, fn, reads=(), writes=()):
        waits = self._collect(eng, reads, writes)
        self.cnt[eng] += 1
        key = ("E", eng)
        ev = (key, self.cnt[eng], eng)
        self.ops[eng].append((waits, fn, (key, 1)))
        self._record(ev, reads, writes)

    def dma(self, queue, name, fn, reads=(), writes=()):
        waits = self._collect(queue, reads, writes)
        key = self._key(("D", name))
        self.dmacnt[key] = self.dmacnt.get(key, 0) + 16
        ev = (key, self.dmacnt[key], "dma")
        self.ops[queue].append((waits, fn, (key, 16)))
        self._record(ev, reads, writes)

    def barrier(self):
        finals = [(("E", e), self.cnt[e]) for e in ENGS if self.cnt[e] > 0]
        finals += list(self.dmacnt.items())
        for e in ENGS:
            self.ops[e].append((finals, None, None))

    def replay(self, nc, sems):
        ops = self.ops
        with nc.Block() as block:
            def run(engname):
                def body(eng):
                    for waits, fn, inc in ops[engname]:
                        for k, v in waits:
                            eng.wait_ge(sems[k], v)
                        if fn is not None:
                            inst = fn(eng)
                            inst.then_inc(sems[inc[0]], inc[1])
                return body
            block.tensor(run("tensor"))
            block.vector(run("vector"))
            block.scalar(run("scalar"))
            block.gpsimd(run("gpsimd"))
            block.sync(run("sync"))

    def next_phase(self):
        self.phase += 1
        self._reset()


class PS:
    def __init__(self, banks):
        self.banks = banks
        self.groups = {}

    def setup(self, **groups):
        self.groups = {k: [list(v), 0] for k, v in groups.items()}

    def get(self, group):
        g = self.groups[group]
        b = g[0][g[1] % len(g[0])]
        g[1] += 1
        return b, self.banks[b], ("ps", b)


def t5_buckets(rel):
    nb = 16
    max_exact = 8
    n = np.abs(rel)
    large = max_exact + (np.log(np.maximum(n, 1) / max_exact)
                         / math.log(1024 / max_exact) * (nb - max_exact)).astype(np.int32)
    large = np.minimum(large, nb - 1)
    return ((rel > 0).astype(np.int32) * nb + np.where(n < max_exact, n, large)).astype(np.int32)


FAMS = [(1, 128), (1, 64), (4, 64), (16, 64)]


def static_consts():
    cst = np.zeros((128, 384), np.float32)
    cst[:, 0:128] = np.eye(128, dtype=np.float32)
    cst[0:64, 128:192] = 1.0
    cst[64:128, 192:256] = 1.0
    cst[0, 320:384] = 1.0
    oht = np.zeros((33, 4, 512), np.float32)
    rel = np.arange(512) - 255
    for f, (d, hw) in enumerate(FAMS):
        b = t5_buckets(d * rel)
        valid = (np.abs(rel) <= hw) & (np.arange(512) < 511)
        for n in range(512):
            if valid[n]:
                oht[b[n], f, n] = 1.0
            else:
                oht[32, f, n] = 1.0
    return cst, oht.reshape(33, 2048)


def build_program(segs, debug=False, nphase=6):
    T = sum(L for _, L in segs)
    spans = []
    for (sb, L) in segs:
        for s in range(L // SPAN):
            spans.append((sb + s * SPAN, sb, L))

    nc = bass.Bass("TRN2", target_bir_lowering=False)

    def din(name, shape, dt=F32):
        return nc.dram_tensor(name, list(shape), dt, kind="ExternalInput")

    x_in = din("x", [T, D])
    wqkvA = din("wqkvA", [128, 8, 1536])
    woA = din("woA", [128, 8, 1024])
    wqkvB = din("wqkvB", [128, 8, 4608])
    woB = din("woB", [128, 8, 1024])
    wgu = [din("wgu0", [128, 8, 2 * DFF]), din("wgu1", [128, 8, 2 * DFF])]
    wdn = [din("wd0", [128, NCH, D]), din("wd1", [128, NCH, D])]
    gn = din("gn", [4, 128, D])
    gqk_in = din("gqk", [128, 8])
    sink_in = din("sink", [1, 16])
    relt_in = din("relt", [32, 16])
    cst_in = din("cst", [128, 384])
    oht_in = din("oht", [33, 2048])
    okind = "ExternalOutput"
    y_out = nc.dram_tensor("y", [T, D], F32, kind=okind)
    ikind = okind if debug else "Internal"
    x1 = nc.dram_tensor("x1", [T, D], F32, kind=ikind)
    x2 = nc.dram_tensor("x2", [T, D], F32, kind=ikind)
    x3 = nc.dram_tensor("x3", [T, D], F32, kind=ikind)
    qA = nc.dram_tensor("qA", [1, 8, 128, T], BF16)
    kA = nc.dram_tensor("kA", [1, 2, 128, T], BF16)
    vA = nc.dram_tensor("vA", [1, T, 512], BF16)
    qB = nc.dram_tensor("qB", [3, 8, 128, T], BF16)
    kB = nc.dram_tensor("kB", [3, 2, 128, T], BF16)
    vB = nc.dram_tensor("vB", [3, T, 512], BF16)
    gv = nc.dram_tensor("gv", [4, 16, 512], F32)

    S = Sched()
    uid = [0]
    sems = {}

    with ExitStack() as gs:
        def galloc(name, shape, dt):
            return gs.enter_context(nc.sbuf_tensor("g_" + name, list(shape), dt))

        dbl = [gs.enter_context(nc.psum_tensor(f"psd{i}", [128, 1024], F32)) for i in range(4)]
        banks = []
        for i in range(4):
            banks.append(dbl[i][:, 0:512])
            banks.append(dbl[i][:, 512:1024])
        ps = PS(banks)
        ident = galloc("ident", [128, 128], BF16)
        blk = galloc("blk", [128, 128], BF16)
        sel = galloc("sel", [1, 128], BF16)
        esrow = galloc("esrow", [1, 2048], BF16)
        gqk = galloc("gqkt", [128, 4], F32)

        def finish_phase():
            S.barrier()
            for k in S.keys:
                if k not in sems:
                    nm = "s_" + "_".join(str(z) for z in k)
                    sems[k] = gs.enter_context(nc.semaphore(nm))
            with nc.allow_non_contiguous_dma(reason="strided scratch layouts"):
                S.replay(nc, sems)
            S.next_phase()

        with ExitStack() as pst:
            def A(name, shape, dt):
                uid[0] += 1
                return pst.enter_context(nc.sbuf_tensor(f"{name}_u{uid[0]}", list(shape), dt))
            ta = A("ta", [33, 16], F32)
            oht = A("oht", [33, 2048], F32)
            gvs = A("gvs", [16, 2048], F32)
            es = A("es", [1, 16], F32)
            es2 = A("es2", [1, 16], F32)
            gq = A("gq", [128, 8], F32)
            gtmp = A("gtmp", [128, 4], F32)

            S.dma("gpsimd", "c0", lambda e: e.dma_start(out=ident[:], in_=cst_in.ap()[:, 0:128]), writes=["ident"])
            S.dma("gpsimd", "c0", lambda e: e.dma_start(out=blk[:], in_=cst_in.ap()[:, 128:256]), writes=["blk"])
            S.dma("gpsimd", "c0", lambda e: e.dma_start(out=sel[:], in_=cst_in.ap()[0:1, 256:384]), writes=["sel"])
            S.dma("sync", "c1", lambda e: e.dma_start(out=ta[0:32, :], in_=relt_in.ap()), writes=["ta0"])
            S.op("vector", lambda e: e.memset(ta[32:33, :], NEGB), writes=["ta1"])
            S.dma("sync", "c1", lambda e: e.dma_start(out=oht[:], in_=oht_in.ap()), writes=["oht"])
            S.dma("sync", "c1", lambda e: e.dma_start(out=es[:], in_=sink_in.ap()), writes=["es"])
            S.dma("sync", "c1", lambda e: e.dma_start(out=gq[:], in_=gqk_in.ap()), writes=["gq"])
            for f in range(4):
                b, bank, bk = (f, banks[f], ("ps", f))
                S.op("tensor", lambda e, f=f, bank=bank: e.matmul(bank[0:16, :], ta[0:33, :], oht[0:33, f * 512:(f + 1) * 512],
                                                                  start=True, stop=True),
                     reads=["ta0", "ta1", "oht"], writes=[bk])
                S.op("vector", lambda e, f=f, bank=bank: e.tensor_copy(out=gvs[:, f * 512:(f + 1) * 512], in_=bank[0:16, :]),
                     reads=[bk], writes=[("gvs", f)])
                S.dma("sync", "c2", lambda e, f=f: e.dma_start(out=gv.ap()[f], in_=gvs[:, f * 512:(f + 1) * 512]),
                      reads=[("gvs", f)], writes=[("gv", f)])
            S.op("scalar", lambda e: e.activation(out=es2[:], in_=es[:], func=AF.Exp), reads=["es"], writes=["es2"])
            S.op("vector", lambda e: e.tensor_copy(out=esrow[:].rearrange("p (h q) -> p h q", q=128),
                                                   in_=bass.AP(es2, 0, [[16, 1], [1, 16], [0, 128]])),
                 reads=["es2"], writes=["esrow"])
            S.op("vector", lambda e: e.tensor_tensor(out=gtmp[:], in0=gq[:].rearrange("p (f t) -> p f t", t=2)[:, :, 0],
                                                     in1=gq[:].rearrange("p (f t) -> p f t", t=2)[:, :, 1], op=ALU.mult),
                 reads=["gq"], writes=["gtmp"])
            S.op("vector", lambda e: e.tensor_scalar(out=gqk[:], in0=gtmp[:], scalar1=0.125, scalar2=None, op0=ALU.mult),
                 reads=["gtmp"], writes=["gqk"])
            finish_phase()

        def phase_qkv(xsrc, wsrc, ngrp, dils, gidx, qdst, kdst, vdst, gqk_col0):
            with ExitStack() as pst:
                def A(name, shape, dt):
                    uid[0] += 1
                    return pst.enter_context(nc.sbuf_tensor(f"{name}_u{uid[0]}", list(shape), dt))
                WC = ngrp * 1536
                w = A("w", [128, 8, WC], BF16)
                gnt = A("gnt", [128, D], F32)
                xs = [A(f"xs{i}", [128, D], F32) for i in range(4)]
                junk = A("junk", [128, D], BF16)
                hb = [A(f"hb{i}", [128, D], BF16) for i in range(2)]
                hTs = [A(f"hT{i}", [128, 8, SPAN], BF16) for i in range(2)]
                ssq = A("ssq", [128, 16], F32)
                std = A("std", [128, 16], F32)
                rstd = A("rstd", [128, 16], F32)
                sq = [A(f"sq{i}", [128, 512], BF16) for i in range(3)]
                stdb = [A(f"stdb{i}", [128, 512], F32) for i in range(3)]
                rstdb = [A(f"rstdb{i}", [128, 512], F32) for i in range(3)]
                stage = [A(f"stage{i}", [128, SPAN], BF16) for i in range(2)]
                nvs = 2 if ngrp == 1 else 1
                vstage = [A(f"vstage{i}", [128, 16, 4, 128], BF16) for i in range(nvs)]
                ps.setup(tp=[0], mm=[1, 2, 3, 4], sb=[5, 6], v=[7, 1, 2, 3, 4])

                S.dma("sync", "gn", lambda e: e.dma_start(out=gnt[:], in_=gn.ap()[gidx]), writes=["gnt"])
                wloaded = set()

                def load_w(gi_, blk_):
                    if (gi_, blk_) in wloaded:
                        return
                    wloaded.add((gi_, blk_))
                    c0_, c1_ = gi_ * 1536 + blk_ * 512, gi_ * 1536 + (blk_ + 1) * 512
                    S.dma("gpsimd", "w0", lambda e, c0_=c0_, c1_=c1_: e.dma_start(out=w[:, :, c0_:c1_], in_=wsrc.ap()[:, :, c0_:c1_]),
                          writes=[("w", gi_, blk_)])
                for i in range(nvs):
                    S.op("gpsimd", lambda e, i=i: e.memset(vstage[i][:, :, :, 64:128], 1.0), writes=[("vst1", i)])
                sidx = [0]
                vidx = [0]
                ucnt = [0]

                qpend = [None]

                def q_stage1(u):
                    b, bank, bk = ps.get("mm")
                    u["bank"], u["bk"] = bank, bk
                    col0, mt = u["col0"], u["mt"]
                    load_w(u["gi"], (col0 % 1536) // 512)
                    hTc, par = u["hT"], u["par"]

                    def mm(e, bank=bank, col0=col0, mt=mt, hTc=hTc):
                        inst = None
                        for kc in range(8):
                            inst = e.matmul(bank[:, :], w[:, kc, col0:col0 + 128], hTc[:, kc, mt * 512:(mt + 1) * 512],
                                            start=(kc == 0), stop=(kc == 7))
                        return inst
                    S.op("tensor", mm, reads=[("w", u["gi"], (col0 % 1536) // 512), ("hT", par, mt)], writes=[bk])
                    uu = ucnt[0] % 3
                    ucnt[0] += 1
                    u["uu"] = uu
                    S.op("scalar", lambda e, bank=bank, uu=uu: e.activation(out=sq[uu][:], in_=bank[:, :], func=AF.Square),
                         reads=[bk], writes=[("sq", uu)])

                def q_stage2(u):
                    bank, bk, uu, d, mt, st, fc, gi = u["bank"], u["bk"], u["uu"], u["d"], u["mt"], u["st"], u["fc"], u["gi"]
                    v2 = uu
                    b2, bank2, bk2 = ps.get("sb")
                    S.op("tensor", lambda e, bank2=bank2, uu=uu: e.matmul(bank2[:, :], blk[:], sq[uu][:], start=True, stop=True),
                         reads=[("sq", uu), "blk"], writes=[bk2])
                    S.op("scalar", lambda e, bank2=bank2, v2=v2: e.activation(out=stdb[v2][:], in_=bank2[:, :], func=AF.Ln,
                                                                             scale=1.0 / 64, bias=epsb[:, 0:1]),
                         reads=[bk2], writes=[("stdb", v2)])
                    S.op("scalar", lambda e, v2=v2: e.activation(out=rstdb[v2][:], in_=stdb[v2][:], func=AF.Exp, scale=-0.5),
                         reads=[("stdb", v2)], writes=[("rstdb", v2)])
                    nt = 512 // d
                    outv = stage[st][:, :].rearrange("p (r t) -> p r t", r=d)[:, :, mt * nt:(mt + 1) * nt]
                    if fc < 8:
                        S.op("vector", lambda e, bank=bank, v2=v2, outv=outv, d=d: e.tensor_tensor(
                            out=outv, in0=bank[:, :].rearrange("p (t r) -> p r t", r=d),
                            in1=rstdb[v2][:, :].rearrange("p (t r) -> p r t", r=d), op=ALU.mult),
                             reads=[bk, ("rstdb", v2)], writes=[("stage", st)])
                    else:
                        gc = gqk_col0 + gi
                        S.op("vector", lambda e, bank=bank, v2=v2, outv=outv, d=d, gc=gc: e.scalar_tensor_tensor(
                            out=outv, in0=bank[:, :].rearrange("p (t r) -> p r t", r=d), scalar=gqk[:, gc:gc + 1],
                            in1=rstdb[v2][:, :].rearrange("p (t r) -> p r t", r=d), op0=ALU.mult, op1=ALU.mult),
                             reads=[bk, ("rstdb", v2), "gqk"], writes=[("stage", st)])
                    if mt == 3:
                        dview = u["dview"]
                        S.dma("gpsimd", f"stg{st}", lambda e, st=st, dview=dview, d=d: e.dma_start(
                            out=dview, in_=stage[st][:, :].rearrange("p (r t) -> p r t", r=d)),
                              reads=[("stage", st)])

                def qpush(u):
                    q_stage1(u)
                    if qpend[0] is not None:
                        q_stage2(qpend[0])
                    qpend[0] = u

                def qflush():
                    if qpend[0] is not None:
                        q_stage2(qpend[0])
                        qpend[0] = None

                def a1(Pb, j):
                    sl = j % 4
                    if j == 0:
                        S.op("gpsimd", lambda e: e.memset(ssq[:], 0.0), writes=[("ssq", jj) for jj in range(16)])
                    S.dma("sync", f"xs{sl}", lambda e, sl=sl, j=j, Pb=Pb: e.dma_start(
                        out=xs[sl][:], in_=xsrc.ap()[Pb + j * 128:Pb + (j + 1) * 128, :]), writes=[("xs", sl)])
                    S.op("scalar", lambda e, sl=sl, j=j: e.activation(out=junk[:], in_=xs[sl][:], func=AF.Square,
                                                                     accum_out=ssq[:, j:j + 1]),
                         reads=[("xs", sl), ("ssq", j)], writes=["junk", ("ssq", j)])
                    S.op("scalar", lambda e, j=j: e.activation(out=std[:, j:j + 1], in_=ssq[:, j:j + 1], func=AF.Ln,
                                                              scale=1.0 / D, bias=epsb[:, 0:1]),
                         reads=[("ssq", j)], writes=[("std", j)])
                    S.op("scalar", lambda e, j=j: e.activation(out=rstd[:, j:j + 1], in_=std[:, j:j + 1], func=AF.Exp, scale=-0.5),
                         reads=[("std", j)], writes=[("rstd", j)])
                    hs = j % 2
                    S.op("vector", lambda e, j=j, sl=sl, hs=hs: e.scalar_tensor_tensor(
                        out=hb[hs][:], in0=xs[sl][:], scalar=rstd[:, j:j + 1], in1=gnt[:], op0=ALU.mult, op1=ALU.mult),
                         reads=[("xs", sl), ("rstd", j), "gnt"], writes=[("hb", hs)])

                def a2(par, j):
                    hs = j % 2
                    hTc = hTs[par]
                    b, bank, bk = ps.get("tp")
                    tpv = bank.bitcast(BF16)

                    def tr(e, hs=hs, tpv=tpv):
                        inst = None
                        for kc in range(8):
                            inst = e.transpose(tpv[:, kc * 128:(kc + 1) * 128], hb[hs][:, kc * 128:(kc + 1) * 128], ident[:])
                        return inst
                    S.op("tensor", tr, reads=[("hb", hs), "ident"], writes=[bk])
                    if j % 2 == 0:
                        S.op("scalar", lambda e, j=j, tpv=tpv, hTc=hTc: e.activation(
                            out=hTc[:, :, j * 128:(j + 1) * 128], in_=tpv[:, :].rearrange("p (k t) -> p k t", k=8), func=AF.Identity),
                             reads=[bk], writes=[("hT", par, j // 4)])
                    else:
                        S.op("vector", lambda e, j=j, tpv=tpv, hTc=hTc: e.tensor_copy(
                            out=hTc[:, :, j * 128:(j + 1) * 128], in_=tpv[:, :].rearrange("p (k t) -> p k t", k=8)),
                             reads=[bk], writes=[("hT", par, j // 4)])

                def a_steps(si_):
                    Pb_ = spans[si_][0]
                    par_ = si_ % 2
                    st_ = []
                    for j in range(16):
                        st_.append(lambda j=j: a1(Pb_, j))
                        if j >= 1:
                            st_.append(lambda j=j: a2(par_, j - 1))
                    st_.append(lambda: a2(par_, 15))
                    return st_

                asteps = []

                def a_emit(n):
                    for _ in range(n):
                        if asteps:
                            asteps.pop(0)()

                for f_ in a_steps(0):
                    f_()
                for si_, (Pb, Sb, L) in enumerate(spans):
                    par = si_ % 2
                    hT = hTs[par]
                    if si_ + 1 < len(spans):
                        asteps.extend(a_steps(si_ + 1))
                    T0s = Pb - Sb
                    for gi in range(ngrp):
                        d = dils[gi]
                        Wd_ = SPAN // d
                        Ls = L // d
                        T0 = T0s // d
                        for fc in range(10):
                            col0 = gi * 1536 + (fc * 128 if fc < 8 else 1024 + (fc - 8) * 128)
                            st = sidx[0] % 2
                            sidx[0] += 1
                            dst = (qdst.ap()[gi, fc] if fc < 8 else kdst.ap()[gi, fc - 8])
                            dview = dst[:, Sb:Sb + L].rearrange("p (r t) -> p r t", r=d)[:, :, T0:T0 + Wd_]
                            for mt in range(4):
                                qpush(dict(col0=col0, mt=mt, fc=fc, st=st, d=d, gi=gi, dview=dview, hT=hT, par=par))
                            a_emit(4)
                        qflush()
                        vs = vidx[0] % nvs
                        vidx[0] += 1
                        nq16 = 16 // d
                        for vt in range(16):
                            r = vt // nq16
                            qq = vt % nq16
                            c0 = qq * 128 * d + r
                            b, bank, bk = ps.get("v")
                            vcol = gi * 1536 + 1280

                            load_w(gi, 2)

                            def vmm(e, bank=bank, c0=c0, d=d, vcol=vcol, hTc=hT):
                                inst = None
                                for kc in range(8):
                                    lhsT = hTc[:, kc, c0:c0 + 127 * d + 1:d] if d > 1 else hTc[:, kc, c0:c0 + 128]
                                    inst = e.matmul(bank[:, 0:256], lhsT, w[:, kc, vcol:vcol + 256], start=(kc == 0), stop=(kc == 7))
                                return inst
                            S.op("tensor", vmm, reads=[("w", gi, 2)] + [("hT", par, i) for i in range(4)], writes=[bk])
                            if vt % 2 == 1:
                                a_emit(2)
                            S.op("vector", lambda e, bank=bank, vs=vs, vt=vt: e.tensor_copy(
                                out=vstage[vs][:, vt, :, 0:64], in_=bank[:, 0:256].rearrange("p (g c) -> p g c", g=4)),
                                 reads=[bk, ("vst1", vs)], writes=[("vstage", vs)])
                        vd = vdst.ap()[gi, Sb:Sb + L, :].rearrange("(r t) c -> r t c", r=d)[:, T0:T0 + Wd_, :]
                        vd = vd.rearrange("r (q p) c -> p r q c", p=128)
                        for r in range(d):
                            S.dma("gpsimd", f"vst{vs}", lambda e, vs=vs, vd=vd, r=r, nq16=nq16: e.dma_start(
                                out=vd[:, r], in_=vstage[vs][:, r * nq16:(r + 1) * nq16, :, :].rearrange("p q g c -> p q (g c)")),
                                  reads=[("vstage", vs)])
                    a_emit(len(asteps))
                finish_phase()

        def build_bt(A, BT, kinds, hk=None, pstride=2048, hkey="hk"):
            if hk is None:
                hk = A("hk", [128, 16, 128], F32)
            for k, (f, off) in enumerate(kinds):
                base = off + 128
                S.dma("sync", "hk", lambda e, f=f, base=base: e.dma_start(
                    out=bass.AP(hk, 0, [[pstride, 128], [128, 16], [1, 128]]),
                    in_=bass.AP(gv, f * 16 * 512 + base, [[1, 128], [512, 16], [1, 128]])), writes=[hkey])
                S.op("vector", lambda e, k=k: e.tensor_copy(out=BT[:, k, :, :], in_=bass.AP(hk, 127, [[pstride, 128], [128, 16], [-1, 128]])),
                     reads=[hkey], writes=["BT"])

        def phase_att_a(xsrc, xdst):
            with ExitStack() as pst:
                def A(name, shape, dt):
                    uid[0] += 1
                    return pst.enter_context(nc.sbuf_tensor(f"{name}_u{uid[0]}", list(shape), dt))
                wo = A("wo", [128, 8, D], BF16)
                BT = A("BT", [128, 3, 16, 128], BF16)
                qT = [A(f"qT{i}", [128, 8, SPAN], BF16) for i in range(2)]
                kT = [A(f"kT{i}", [128, 4, SPAN + 256], BF16) for i in range(2)]
                va = [A(f"va{i}", [128, 18, 512], BF16) for i in range(2)]
                xs = [A(f"xs{i}", [128, D], F32) for i in range(3)]
                pt = [A(f"pt{i}", [128, 512], BF16) for i in range(4)]
                rden = [A(f"rden{i}", [128, 512], F32) for i in range(2)]
                lnd = [A(f"lnd{i}", [128, 512], F32) for i in range(2)]
                oT = [A(f"oT{i}", [128, 8, 128], BF16) for i in range(2)]
                ps.setup(st=[0, 2, 4], ot=[6, 7], wo=[6, 7])
                for kc in range(8):
                    S.dma("gpsimd", "w", lambda e, kc=kc: e.dma_start(out=wo[:, kc, :], in_=woA.ap()[:, kc, :]), writes=[("wo", kc)])
                build_bt(A, BT, [(0, -128), (0, 0), (0, 128)])
                wokeys = [("wo", kc) for kc in range(8)]
                pti = [0]
                xi = [0]
                oi = [0]
                ri = [0]
                for si, (Pb, Sb, L) in enumerate(spans):
                    bf = si % 2
                    lo = max(Sb, Pb - 128)
                    hi = min(Sb + L, Pb + SPAN + 128)
                    c_lo = lo - (Pb - 128)
                    n = hi - lo
                    S.dma("sync", f"q{bf}", lambda e, bf=bf, Pb=Pb: e.dma_start(
                        out=qT[bf][:, 0:4, :], in_=qA.ap()[0, 0:4, :, Pb:Pb + SPAN].rearrange("c p t -> p c t")), writes=[("qT", bf)])
                    S.dma("sync", f"q{bf}", lambda e, bf=bf, Pb=Pb: e.dma_start(
                        out=qT[bf][:, 4:8, :], in_=qA.ap()[0, 4:8, :, Pb:Pb + SPAN].rearrange("c p t -> p c t")), writes=[("qT2", bf)])
                    for g in range(4):
                        for half in range(2):
                            S.dma("sync", f"k{bf}", lambda e, bf=bf, g=g, half=half, lo=lo, n=n, c_lo=c_lo: e.dma_start(
                                out=kT[bf][half * 64:(half + 1) * 64, g, c_lo:c_lo + n],
                                in_=kA.ap()[0, g // 2, (g % 2) * 64:(g % 2) * 64 + 64, lo:lo + n]), writes=[("kT", bf, g, half)])
                    kt_lo = c_lo // 128
                    nkt = n // 128
                    S.dma("sync", f"v{bf}", lambda e, bf=bf, lo=lo, n=n, kt_lo=kt_lo, nkt=nkt: e.dma_start(
                        out=va[bf][:, kt_lo:kt_lo + nkt, :],
                        in_=vA.ap()[0, lo:lo + n, :].rearrange("(k p) c -> p k c", p=128)), writes=[("va", bf)])
                    LV = 4
                    pend = []
                    deferred = []

                    def a_stage1(u, bf=bf):
                        g, kt, kind, j = u["g"], u["kt"], u["kind"], u["j"]
                        sb_, sbX, sbkX = ps.get("st")
                        sbY = banks[sb_ + 1]
                        sbkY = ("ps", sb_ + 1)

                        def smm(e, sbX=sbX, sbY=sbY, kind=kind, g=g, bf=bf, kt=kt, j=j):
                            btv = BT[:, kind, 4 * g:4 * g + 4, :].rearrange("p (c h) q -> p c h q", h=2)
                            e.matmul(sbX[:, 0:256], ident[:], btv[:, :, 0, :], start=True, stop=False)
                            e.matmul(sbY[:, 0:256], ident[:], btv[:, :, 1, :], start=True, stop=False)
                            inst = None
                            for hh in range(4):
                                half = hh % 2
                                c = hh // 2
                                ch = 2 * g + c
                                bank = sbX if half == 0 else sbY
                                inst = e.matmul(bank[:, c * 128:(c + 1) * 128],
                                                kT[bf][half * 64:(half + 1) * 64, g, kt * 128:(kt + 1) * 128],
                                                qT[bf][half * 64:(half + 1) * 64, ch, j * 128:(j + 1) * 128],
                                                start=False, stop=(c == 1))
                            return inst
                        S.op("tensor", smm, reads=["BT", "ident", ("qT", bf), ("qT2", bf), ("kT", bf, g, 0), ("kT", bf, g, 1)],
                             writes=[sbkX, sbkY])
                        p_ = pti[0] % 4
                        pti[0] += 1
                        u["p_"] = p_

                        def pexp(e, sb_=sb_, p_=p_):
                            ptv = pt[p_][:, :].rearrange("p (c h q) -> p c h q", c=2, h=2)
                            e.activation(out=ptv[:, :, 0, :], in_=banks[sb_][:, 0:256].rearrange("p (c q) -> p c q", c=2), func=AF.Exp)
                            return e.activation(out=ptv[:, :, 1, :], in_=banks[sb_ + 1][:, 0:256].rearrange("p (c q) -> p c q", c=2), func=AF.Exp)
                        S.op("scalar", pexp, reads=[sbkX, sbkY], writes=[("pt", p_)])

                    def a_stage2(u, bf=bf, Pb=Pb):
                        g, kt, j, ui, p_ = u["g"], u["kt"], u["j"], u["ui"], u["p_"]
                        obank, obk, osl, xsl = u["obank"], u["obk"], u["osl"], u["xsl"]
                        S.op("tensor", lambda e, obank=obank, bf=bf, kt=kt, g=g, p_=p_, ui=ui: e.matmul(
                            obank[:, :], va[bf][:, kt, g * 128:(g + 1) * 128], pt[p_][:], start=(ui == 0), stop=False),
                             reads=[("va", bf), ("pt", p_)], writes=[obk])
                        if not u["last"]:
                            return
                        S.op("tensor", lambda e, obank=obank, g=g: e.matmul(
                            obank[:, :], sel[0:1, :], esrow[0:1, g * 512:(g + 1) * 512], start=False, stop=True),
                             reads=["sel", "esrow"], writes=[obk])
                        r_ = ri[0] % 2
                        ri[0] += 1
                        S.op("scalar", lambda e, obank=obank, r_=r_: e.activation(out=lnd[r_][64:128, :], in_=obank[64:128, :], func=AF.Ln),
                             reads=[obk], writes=[("lnd", r_)])
                        S.op("scalar", lambda e, r_=r_: e.activation(out=rden[r_][64:128, :], in_=lnd[r_][64:128, :], func=AF.Exp, scale=-1.0),
                             reads=[("lnd", r_)], writes=[("rden", r_)])
                        for half in range(2):
                            S.op("vector", lambda e, obank=obank, r_=r_, half=half, g=g, osl=osl: e.tensor_tensor(
                                out=oT[osl][half * 64:(half + 1) * 64, 2 * g:2 * g + 2, :],
                                in0=obank[0:64, :].rearrange("p (c h q) -> p c h q", c=2, h=2)[:, :, half, :],
                                in1=rden[r_][64:128, :].rearrange("p (c h q) -> p c h q", c=2, h=2)[:, :, half, :],
                                op=ALU.mult),
                                 reads=[obk, ("rden", r_)], writes=[("oT", osl)])
                        if g != 3:
                            return

                        def wo_section(osl=osl, xsl=xsl, j=j):
                            sbw, _, _ = ps.get("st")
                            for nn in range(2):
                                wbank, wbk = banks[sbw + nn], ("ps", sbw + nn)

                                def womm(e, wbank=wbank, osl=osl, nn=nn):
                                    inst = None
                                    for c in range(8):
                                        inst = e.matmul(wbank[:, :], oT[osl][:, c, :], wo[:, c, nn * 512:(nn + 1) * 512],
                                                        start=(c == 0), stop=(c == 7))
                                    return inst
                                S.op("tensor", womm, reads=wokeys + [("oT", osl)], writes=[wbk])
                                S.op("vector", lambda e, wbank=wbank, xsl=xsl, nn=nn: e.tensor_tensor(
                                    out=xs[xsl][:, nn * 512:(nn + 1) * 512], in0=wbank[:, :], in1=xs[xsl][:, nn * 512:(nn + 1) * 512],
                                    op=ALU.add), reads=[wbk, ("xs", xsl)], writes=[("xs", xsl)])
                            S.dma("gpsimd", f"xo{xsl}", lambda e, xsl=xsl, j=j, Pb=Pb: e.dma_start(
                                out=xdst.ap()[Pb + j * 128:Pb + (j + 1) * 128, :], in_=xs[xsl][:]), reads=[("xs", xsl)])
                        deferred.append([2, wo_section])

                    def a_push(u):
                        a_stage1(u)
                        pend.append(u)
                        if len(pend) > 2:
                            a_stage2(pend.pop(0))
                        for dfr in list(deferred):
                            dfr[0] -= 1
                            if dfr[0] <= 0:
                                deferred.remove(dfr)
                                dfr[1]()

                    for j in range(16):
                        xsl = xi[0] % 3
                        xi[0] += 1
                        S.dma("sync", f"xs{xsl}", lambda e, xsl=xsl, j=j, Pb=Pb: e.dma_start(
                            out=xs[xsl][:], in_=xsrc.ap()[Pb + j * 128:Pb + (j + 1) * 128, :]), writes=[("xs", xsl)])
                        osl = oi[0] % 2
                        oi[0] += 1
                        for g in range(4):
                            units = []
                            for kind in range(3):
                                kt = j + kind
                                pos = Pb - 128 + kt * 128
                                if Sb <= pos < Sb + L:
                                    units.append((kt, kind))
                            ob, obank, obk = ps.get("ot")
                            for ui, (kt, kind) in enumerate(units):
                                a_push(dict(j=j, g=g, kt=kt, kind=kind, ui=ui, last=(ui == len(units) - 1),
                                            obank=obank, obk=obk, osl=osl, xsl=xsl))
                    while pend:
                        a_stage2(pend.pop(0))
                    for dfr in list(deferred):
                        dfr[1]()
                    deferred.clear()
                finish_phase()

        def phase_ffn(xsrc, xdst, wgu_d, wd_d, gidx):
            with ExitStack() as pst:
                def A(name, shape, dt):
                    uid[0] += 1
                    return pst.enter_context(nc.sbuf_tensor(f"{name}_u{uid[0]}", list(shape), dt))
                wg = A("wg", [128, 8, 2 * DFF], BF16)
                wd = A("wd", [128, NCH, D], BF16)
                gnt = A("gnt", [128, D], F32)
                xs = [A(f"xs{i}", [128, D], F32) for i in range(6)]
                junk = A("junk", [128, D], BF16)
                hb = [A(f"hb{i}", [128, D], BF16) for i in range(2)]
                hT = A("hT", [128, 8, 512], BF16)
                act = A("act", [128, NCH, 512], BF16)
                sg = [A(f"sg{i}", [128, 512], F32) for i in range(2)]
                ssq = A("ssq", [128, 4], F32)
                std = A("std", [128, 4], F32)
                rstd = A("rstd", [128, 4], F32)
                ps.setup(tp=[0, 5], g=[1, 2], u=[3, 4], dn=[6, 7])
                S.dma("sync", "gn", lambda e: e.dma_start(out=gnt[:], in_=gn.ap()[gidx]), writes=["gnt"])
                def load_wg(c):
                    for (nm, col) in (("wgg", c * 128), ("wgu", DFF + c * 128)):
                        S.dma("gpsimd", "w0", lambda e, col=col: e.dma_start(out=wg[:, :, col:col + 128], in_=wgu_d.ap()[:, :, col:col + 128]),
                              writes=[(nm, c)])

                def load_wd():
                    for c in range(NCH):
                        S.dma("gpsimd", "w0", lambda e, c=c: e.dma_start(out=wd[:, c, :], in_=wd_d.ap()[:, c, :]), writes=[("wd", c)])
                wdkeys = [("wd", c) for c in range(NCH)]
                nmt = T // 512
                si = [0]
                tpi = [0]

                def stage_a(m, j):
                    P0 = m * 512
                    sl = (4 * m + j) % 6
                    if j == 0:
                        S.op("gpsimd", lambda e: e.memset(ssq[:], 0.0), writes=[("ssq", jj) for jj in range(4)])
                    S.dma("sync", f"xs{sl}", lambda e, sl=sl, j=j, P0=P0: e.dma_start(
                        out=xs[sl][:], in_=xsrc.ap()[P0 + j * 128:P0 + (j + 1) * 128, :]), writes=[("xs", sl)])
                    S.op("scalar", lambda e, sl=sl, j=j: e.activation(out=junk[:], in_=xs[sl][:], func=AF.Square,
                                                                     accum_out=ssq[:, j:j + 1]),
                         reads=[("xs", sl), ("ssq", j)], writes=["junk", ("ssq", j)])
                    S.op("scalar", lambda e, j=j: e.activation(out=std[:, j:j + 1], in_=ssq[:, j:j + 1], func=AF.Ln,
                                                              scale=1.0 / D, bias=epsb[:, 0:1]),
                         reads=[("ssq", j)], writes=[("std", j)])
                    S.op("scalar", lambda e, j=j: e.activation(out=rstd[:, j:j + 1], in_=std[:, j:j + 1], func=AF.Exp, scale=-0.5),
                         reads=[("std", j)], writes=[("rstd", j)])
                    hs = tpi[0] % 2
                    tpi[0] += 1
                    S.op("vector", lambda e, j=j, sl=sl, hs=hs: e.scalar_tensor_tensor(
                        out=hb[hs][:], in0=xs[sl][:], scalar=rstd[:, j:j + 1], in1=gnt[:], op0=ALU.mult, op1=ALU.mult),
                         reads=[("xs", sl), ("rstd", j), "gnt"], writes=[("hb", hs)])
                    b, bank, bk = ps.get("tp")
                    tpv = bank.bitcast(BF16)

                    def tr(e, hs=hs, tpv=tpv):
                        inst = None
                        for kc in range(8):
                            inst = e.transpose(tpv[:, kc * 128:(kc + 1) * 128], hb[hs][:, kc * 128:(kc + 1) * 128], ident[:])
                        return inst
                    S.op("tensor", tr, reads=[("hb", hs), "ident"], writes=[bk])
                    S.op("vector", lambda e, j=j, tpv=tpv: e.tensor_copy(
                        out=hT[:, :, j * 128:(j + 1) * 128], in_=tpv[:, :].rearrange("p (k t) -> p k t", k=8)),
                         reads=[bk], writes=["hT"])

                for j in range(4):
                    stage_a(0, j)
                for m in range(nmt):
                    P0 = m * 512
                    for c in range(NCH):
                        if m == 0:
                            load_wg(c)
                            if c == NCH - 1:
                                load_wd()
                        gb, gbank, gbk = ps.get("g")
                        ub, ubank, ubk = ps.get("u")

                        def gmm(e, bank=gbank, col=c * 128):
                            inst = None
                            for kc in range(8):
                                inst = e.matmul(bank[:, :], wg[:, kc, col:col + 128], hT[:, kc, :], start=(kc == 0), stop=(kc == 7))
                            return inst
                        S.op("tensor", gmm, reads=[("wgg", c), "hT"], writes=[gbk])

                        def umm(e, bank=ubank, col=DFF + c * 128):
                            inst = None
                            for kc in range(8):
                                inst = e.matmul(bank[:, :], wg[:, kc, col:col + 128], hT[:, kc, :], start=(kc == 0), stop=(kc == 7))
                            return inst
                        S.op("tensor", umm, reads=[("wgu", c), "hT"], writes=[ubk])
                        s_ = si[0] % 2
                        si[0] += 1
                        S.op("scalar", lambda e, gbank=gbank, s_=s_: e.activation(out=sg[s_][:], in_=gbank[:, :], func=AF.Silu),
                             reads=[gbk], writes=[("sg", s_)])
                        S.op("vector", lambda e, ubank=ubank, s_=s_, c=c: e.tensor_tensor(
                            out=act[:, c, :], in0=ubank[:, :], in1=sg[s_][:], op=ALU.mult),
                             reads=[ubk, ("sg", s_)], writes=[("act", c)])
                    for j in range(4):
                        sl = (4 * m + j) % 6
                        for nn in range(2):
                            db, dbank, dbk = ps.get("dn")

                            def dmm(e, bank=dbank, j=j, nn=nn):
                                inst = None
                                for c in range(NCH):
                                    inst = e.matmul(bank[:, :], act[:, c, j * 128:(j + 1) * 128], wd[:, c, nn * 512:(nn + 1) * 512],
                                                    start=(c == 0), stop=(c == NCH - 1))
                                return inst
                            S.op("tensor", dmm, reads=wdkeys + [("act", c) for c in range(NCH)], writes=[dbk])
                            S.op("vector", lambda e, dbank=dbank, sl=sl, nn=nn: e.tensor_tensor(
                                out=xs[sl][:, nn * 512:(nn + 1) * 512], in0=dbank[:, :], in1=xs[sl][:, nn * 512:(nn + 1) * 512],
                                op=ALU.add), reads=[dbk, ("xs", sl)], writes=[("xs", sl)])
                        S.dma("gpsimd", f"xo{sl}", lambda e, sl=sl, j=j, P0=P0: e.dma_start(
                            out=xdst.ap()[P0 + j * 128:P0 + (j + 1) * 128, :], in_=xs[sl][:]), reads=[("xs", sl)])
                        if m + 1 < nmt and j >= 1:
                            stage_a(m + 1, j - 1)
                    if m + 1 < nmt:
                        stage_a(m + 1, 3)
                finish_phase()

        def phase_att_b(xsrc, xdst):
            dils = [1, 4, 16]
            with ExitStack() as pst:
                def A(name, shape, dt):
                    uid[0] += 1
                    return pst.enter_context(nc.sbuf_tensor(f"{name}_u{uid[0]}", list(shape), dt))
                wo = A("wo", [128, 8, D], BF16)
                BT = A("BT", [128, 6, 16, 128], BF16)
                acc = A("acc", [128, 4, SPAN], F32)
                rd = [A(f"rd{i}", [128, 512], F32) for i in range(3)]
                lnb = [A(f"lnb{i}", [128, 512], F32) for i in range(2)]
                oT = A("oT", [128, 8, SPAN], BF16)
                qT = [A(f"qT{i}", [128, 2, SPAN], BF16) for i in range(3)]
                kT = [A(f"kT{i}", [128, 4096], BF16) for i in range(3)]
                va = [A(f"va{i}", [128, 32, 128], BF16) for i in range(3)]
                xs = [A(f"xs{i}", [128, D], F32) for i in range(3)]
                pt = [A(f"pt{i}", [128, 512], BF16) for i in range(4)]
                ps.setup(st=[0, 2, 4], ot=[6, 7], wo=[6, 7])
                for kc in range(8):
                    S.dma("gpsimd", "w", lambda e, kc=kc: e.dma_start(out=wo[:, kc, :], in_=woB.ap()[:, kc, :]), writes=[("wo", kc)])
                build_bt(A, BT, [(1, -64), (1, 64), (2, -64), (2, 64), (3, -64), (3, 64)], hk=acc, pstride=4 * SPAN, hkey="acc")
                for i in range(3):
                    S.op("gpsimd", lambda e, i=i: e.memset(kT[i][:], 0.0), writes=[("kT", i)])
                wokeys = [("wo", kc) for kc in range(8)]
                pti = [0]
                xi = [0]
                li = [0]
                bpend = []
                npend = []

                def b_stage1(u):
                    gi, kk, g, bf, kc0, qc0 = u["gi"], u["kk"], u["g"], u["bf"], u["kc0"], u["qc0"]
                    sb_, sbX, sbkX = ps.get("st")
                    sbY = banks[sb_ + 1]
                    sbkY = ("ps", sb_ + 1)

                    def smm(e, sbX=sbX, sbY=sbY, gi=gi, kk=kk, g=g, bf=bf, kc0=kc0, qc0=qc0):
                        btv = BT[:, 2 * gi + kk, 4 * g:4 * g + 4, :].rearrange("p (c h) q -> p c h q", h=2)
                        e.matmul(sbX[:, 0:256], ident[:], btv[:, :, 0, :], start=True, stop=False)
                        e.matmul(sbY[:, 0:256], ident[:], btv[:, :, 1, :], start=True, stop=False)
                        inst = None
                        for hh in range(4):
                            half = hh % 2
                            c = hh // 2
                            bank = sbX if half == 0 else sbY
                            inst = e.matmul(bank[:, c * 128:(c + 1) * 128],
                                            kT[bf][half * 64:(half + 1) * 64, kc0:kc0 + 128],
                                            qT[bf][half * 64:(half + 1) * 64, c, qc0:qc0 + 128],
                                            start=False, stop=(c == 1))
                        return inst
                    S.op("tensor", smm, reads=["BT", "ident", ("qT", bf, 0), ("qT", bf, 1), ("kTl", bf, 0), ("kTl", bf, 1)],
                         writes=[sbkX, sbkY])
                    p_ = pti[0] % 4
                    pti[0] += 1
                    u["p_"] = p_

                    def pexp(e, sb_=sb_, p_=p_):
                        ptv = pt[p_][:, :].rearrange("p (c h q) -> p c h q", c=2, h=2)
                        e.activation(out=ptv[:, :, 0, :], in_=banks[sb_][:, 0:256].rearrange("p (c q) -> p c q", c=2), func=AF.Exp)
                        return e.activation(out=ptv[:, :, 1, :], in_=banks[sb_ + 1][:, 0:256].rearrange("p (c q) -> p c q", c=2), func=AF.Exp)
                    S.op("scalar", pexp, reads=[sbkX, sbkY], writes=[("pt", p_)])

                def b_stage2(u):
                    gi, kk, bf, ti, p_ = u["gi"], u["kk"], u["bf"], u["ti"], u["p_"]
                    obank, obk, accv = u["obank"], u["obk"], u["accv"]
                    S.op("tensor", lambda e, obank=obank, bf=bf, ti=ti, p_=p_, kk=kk: e.matmul(
                        obank[:, :], va[bf][:, ti, :], pt[p_][:], start=(kk == 0), stop=(kk == 1)),
                         reads=u["vkeys"] + [("pt", p_)], writes=[obk])
                    if kk == 0:
                        return
                    pk = u["pk"]
                    if gi == 0:
                        S.op("vector", lambda e, obank=obank, accv=accv: e.tensor_copy(
                            out=accv, in_=obank[:, :].rearrange("p (h q) -> p h q", h=4)),
                             reads=[obk], writes=["acc"] + pk)
                    else:
                        S.op("vector", lambda e, obank=obank, accv=accv: e.tensor_tensor(
                            out=accv, in0=obank[:, :].rearrange("p (h q) -> p h q", h=4), in1=accv, op=ALU.add),
                             reads=[obk] + pk, writes=["acc"] + pk)

                def b_push(u):
                    b_stage1(u)
                    bpend.append(u)
                    if len(bpend) > 2:
                        b_stage2(bpend.pop(0))

                def b_flush():
                    while bpend:
                        b_stage2(bpend.pop(0))

                for (Pb, Sb, L) in spans:
                    for g in range(4):
                        for gi in range(3):
                            d = dils[gi]
                            W_ = SPAN // d
                            Ls = L // d
                            T0 = (Pb - Sb) // d
                            nq = W_ // 128
                            KW = W_ + 128
                            bf = li[0] % 3
                            li[0] += 1
                            for c in range(2):
                                src = qB.ap()[gi, 2 * g + c, :, Sb:Sb + L].rearrange("p (r t) -> p r t", r=d)[:, :, T0:T0 + W_]
                                S.dma("sync", f"q{bf}", lambda e, bf=bf, c=c, src=src, d=d: e.dma_start(
                                    out=qT[bf][:, c, :].rearrange("p (r t) -> p r t", r=d), in_=src), writes=[("qT", bf, c)])
                            tlo = max(0, T0 - 64)
                            thi = min(Ls, T0 + W_ + 64)
                            koff = tlo - (T0 - 64)
                            kn = thi - tlo
                            for half in range(2):
                                src = kB.ap()[gi, g // 2, (g % 2) * 64:(g % 2) * 64 + 64, Sb:Sb + L].rearrange(
                                    "p (r t) -> p r t", r=d)[:, :, tlo:thi]
                                S.dma("sync", f"k{bf}", lambda e, bf=bf, half=half, src=src, d=d, KW=KW, koff=koff, kn=kn: e.dma_start(
                                    out=kT[bf][half * 64:(half + 1) * 64, 0:d * KW].rearrange("p (r t) -> p r t", r=d)[:, :, koff:koff + kn],
                                    in_=src), writes=[("kTl", bf, half)], reads=[("kT", bf)])
                            vkeys = []
                            for r in range(d):
                                m0 = 0
                                m1 = nq + 1
                                if T0 == 0:
                                    ti = r * (nq + 1)
                                    S.op("vector", lambda e, bf=bf, ti=ti: e.memset(va[bf][0:64, ti, :], 0.0), writes=[("va", bf, r, "z0")])
                                    row0 = Sb + r * Ls
                                    S.dma("sync", f"v{bf}", lambda e, bf=bf, ti=ti, row0=row0, gi=gi, g=g: e.dma_start(
                                        out=va[bf][64:128, ti, :], in_=vB.ap()[gi, row0:row0 + 64, g * 128:(g + 1) * 128]),
                                          writes=[("va", bf, r, "e0")])
                                    vkeys += [("va", bf, r, "z0"), ("va", bf, r, "e0")]
                                    m0 = 1
                                if T0 + W_ == Ls:
                                    ti = r * (nq + 1) + nq
                                    S.op("vector", lambda e, bf=bf, ti=ti: e.memset(va[bf][64:128, ti, :], 0.0), writes=[("va", bf, r, "z1")])
                                    row0 = Sb + r * Ls + Ls - 64
                                    S.dma("sync", f"v{bf}", lambda e, bf=bf, ti=ti, row0=row0, gi=gi, g=g: e.dma_start(
                                        out=va[bf][0:64, ti, :], in_=vB.ap()[gi, row0:row0 + 64, g * 128:(g + 1) * 128]),
                                          writes=[("va", bf, r, "e1")])
                                    vkeys += [("va", bf, r, "z1"), ("va", bf, r, "e1")]
                                    m1 = nq
                                if m1 > m0:
                                    row0 = Sb + r * Ls + T0 - 64 + m0 * 128
                                    nm = m1 - m0
                                    ti = r * (nq + 1) + m0
                                    S.dma("sync", f"v{bf}", lambda e, bf=bf, ti=ti, nm=nm, row0=row0, gi=gi, g=g: e.dma_start(
                                        out=va[bf][:, ti:ti + nm, :],
                                        in_=vB.ap()[gi, row0:row0 + nm * 128, g * 128:(g + 1) * 128].rearrange("(m p) c -> p m c", p=128)),
                                          writes=[("va", bf, r, "m")])
                                    vkeys.append(("va", bf, r, "m"))
                            for r in range(d):
                                for tq in range(nq):
                                    ob, obank, obk = ps.get("ot")
                                    accv = acc[:, :, :].rearrange("p h (t r) -> p h t r", r=d)[:, :, tq * 128:(tq + 1) * 128, r]
                                    if d == 1:
                                        while npend and npend[0][0] <= tq:
                                            npend.pop(0)[1]()
                                        pk = [("accp", tq // 4)]
                                    elif d == 4:
                                        pk = [("accp", tq)]
                                    else:
                                        pk = [("accp", i_) for i_ in range(4)]
                                    for kk in range(2):
                                        m = tq + kk
                                        b_push(dict(gi=gi, kk=kk, g=g, bf=bf, kc0=r * KW + m * 128, qc0=r * W_ + tq * 128,
                                                    ti=r * (nq + 1) + m, obank=obank, obk=obk, vkeys=vkeys, accv=accv, pk=pk))
                        b_flush()
                        def norm_piece(pc, g=g):
                            cs = slice(pc * 512, (pc + 1) * 512)
                            for hh in range(4):
                                half = hh % 2
                                nb_ = (pc * 4 + hh) % 2
                                if (pc * 4 + hh) % 5 == 2:
                                    nb_ = 2
                                    S.op("vector", lambda e, hh=hh, cs=cs: e.reciprocal(out=rd[2][0:64, :], in_=acc[64:128, hh, cs]),
                                         reads=[("accp", pc)], writes=[("rd", 2)])
                                else:
                                    S.op("scalar", lambda e, hh=hh, cs=cs, nb_=nb_: e.activation(out=lnb[nb_][64:128, :], in_=acc[64:128, hh, cs], func=AF.Ln),
                                         reads=[("accp", pc)], writes=[("lnb", nb_)])
                                    S.op("scalar", lambda e, nb_=nb_: e.activation(out=rd[nb_][0:64, :], in_=lnb[nb_][64:128, :], func=AF.Exp, scale=-1.0),
                                         reads=[("lnb", nb_)], writes=[("rd", nb_)])
                                S.op("vector", lambda e, hh=hh, half=half, g=g, cs=cs, nb_=nb_: e.tensor_tensor(
                                    out=oT[half * 64:(half + 1) * 64, 2 * g + hh // 2, cs], in0=acc[0:64, hh, cs], in1=rd[nb_][0:64, :], op=ALU.mult),
                                     reads=[("accp", pc), ("rd", nb_)], writes=[("oT", g)])
                        norm_piece(0)
                        if g < 3:
                            npend.extend([(2, lambda f=norm_piece: f(1)), (6, lambda f=norm_piece: f(2)), (10, lambda f=norm_piece: f(3))])
                        else:
                            for pc in range(1, 4):
                                norm_piece(pc)
                    for j in range(16):
                        xsl = xi[0] % 3
                        xi[0] += 1
                        S.dma("sync", f"xs{xsl}", lambda e, xsl=xsl, j=j, Pb=Pb: e.dma_start(
                            out=xs[xsl][:], in_=xsrc.ap()[Pb + j * 128:Pb + (j + 1) * 128, :]), writes=[("xs", xsl)])
                        for nn in range(2):
                            wb, wbank, wbk = ps.get("wo")

                            def womm(e, wbank=wbank, j=j, nn=nn):
                                inst = None
                                for c in range(8):
                                    inst = e.matmul(wbank[:, :], oT[:, c, j * 128:(j + 1) * 128], wo[:, c, nn * 512:(nn + 1) * 512],
                                                    start=(c == 0), stop=(c == 7))
                                return inst
                            S.op("tensor", womm, reads=wokeys + [("oT", g_) for g_ in range(4)], writes=[wbk])
                            S.op("vector", lambda e, wbank=wbank, xsl=xsl, nn=nn: e.tensor_tensor(
                                out=xs[xsl][:, nn * 512:(nn + 1) * 512], in0=wbank[:, :], in1=xs[xsl][:, nn * 512:(nn + 1) * 512],
                                op=ALU.add), reads=[wbk, ("xs", xsl)], writes=[("xs", xsl)])
                        S.dma("gpsimd", f"xo{xsl}", lambda e, xsl=xsl, j=j, Pb=Pb: e.dma_start(
                            out=xdst.ap()[Pb + j * 128:Pb + (j + 1) * 128, :], in_=xs[xsl][:]), reads=[("xs", xsl)])
                finish_phase()

        epsb = galloc("epsb", [128, 1], F32)
        S.op("vector", lambda e: e.memset(epsb[:], EPS), writes=["epsb"])
        finish_phase()

        if nphase >= 1:
            phase_qkv(x_in, wqkvA, 1, [1], 0, qA, kA, vA, 0)
        if nphase >= 2:
            phase_att_a(x_in, x1)
        if nphase >= 3:
            phase_ffn(x1, x2, wgu[0], wdn[0], 1)
        if nphase >= 4:
            phase_qkv(x2, wqkvB, 3, [1, 4, 16], 2, qB, kB, vB, 1)
        if nphase >= 5:
            phase_att_b(x2, x3)
        if nphase >= 6:
            phase_ffn(x3, y_out, wgu[1], wdn[1], 3)
    return nc


def _kc_layout(w):
    k, f = w.shape
    return np.ascontiguousarray(w.reshape(k // 128, 128, f).transpose(1, 0, 2))


def shared_inputs(rel_table, norm_attn, norm_ffn, a_w_qkv, a_q_gain, a_k_gain, a_sink, a_w_o,
                  b_w_qkv, b_q_gain, b_k_gain, b_w_o, ffn_w_gate_up, ffn_w_down):
    f = lambda a: np.asarray(a, dtype=np.float32)
    cst, oht = static_consts()
    gn = np.stack([np.broadcast_to(f(norm_attn)[0], (128, D)), np.broadcast_to(f(norm_ffn)[0], (128, D)),
                   np.broadcast_to(f(norm_attn)[1], (128, D)), np.broadcast_to(f(norm_ffn)[1], (128, D))]).copy()
    rep = lambda v: np.tile(f(v).reshape(64), 2)
    gqk = np.stack([rep(a_q_gain[0]), rep(a_k_gain[0]),
                    rep(b_q_gain[0][0]), rep(b_k_gain[0][0]),
                    rep(b_q_gain[0][1]), rep(b_k_gain[0][1]),
                    rep(b_q_gain[0][2]), rep(b_k_gain[0][2])], axis=1).copy()
    return {
        "wqkvA": _kc_layout(f(a_w_qkv)[0]), "woA": _kc_layout(f(a_w_o)[0]),
        "wqkvB": _kc_layout(f(b_w_qkv)[0]), "woB": _kc_layout(f(b_w_o)[0]),
        "wgu0": _kc_layout(f(ffn_w_gate_up)[0]), "wgu1": _kc_layout(f(ffn_w_gate_up)[1]),
        "wd0": _kc_layout(f(ffn_w_down)[0]), "wd1": _kc_layout(f(ffn_w_down)[1]),
        "gn": gn, "gqk": gqk, "sink": f(a_sink).reshape(1, 16), "relt": f(rel_table),
        "cst": cst, "oht": oht,
    }


_PROGRAM = {}


def kernel(x_prompt, x_sample, rel_table, norm_attn, norm_ffn, a_w_qkv, a_q_gain, a_k_gain,
           a_sink, a_w_o, b_w_qkv, b_q_gain, b_k_gain, b_w_o, ffn_w_gate_up, ffn_w_down):
    x_prompt = np.asarray(x_prompt, dtype=np.float32)
    x_sample = np.asarray(x_sample, dtype=np.float32)
    shared = shared_inputs(rel_table, norm_attn, norm_ffn, a_w_qkv, a_q_gain, a_k_gain, a_sink, a_w_o,
                           b_w_qkv, b_q_gain, b_k_gain, b_w_o, ffn_w_gate_up, ffn_w_down)
    n = 8
    in_maps = []
    for c in range(n):
        xc = np.concatenate([x_sample[c], x_prompt[2 * c], x_prompt[2 * c + 1]], axis=0)
        m = dict(shared)
        m["x"] = np.ascontiguousarray(xc)
        in_maps.append(m)
    if "full" not in _PROGRAM:
        _PROGRAM["full"] = build_program(FULL_SEGS)
    nc = _PROGRAM["full"]
    res = run_bass_kernel_spmd(nc, in_maps, core_ids=list(range(n)))
    y_prompt = np.empty_like(x_prompt)
    y_sample = np.empty_like(x_sample)
    for c in range(n):
        y = np.asarray(res.results[c]["y"], dtype=np.float32)
        y_sample[c] = y[0:8192]
        y_prompt[2 * c] = y[8192:10240]
        y_prompt[2 * c + 1] = y[10240:12288]
    return (y_prompt, y_sample)
```

```python
import math
import os
from contextlib import ExitStack

import numpy as np
import concourse.bass as bass
import concourse.mybir as mybir
from concourse.bass_utils import run_bass_kernel_spmd

F32 = mybir.dt.float32
BF16 = mybir.dt.bfloat16
AF = mybir.ActivationFunctionType
ALU = mybir.AluOpType

D = 1024
DFF = 2816
NCH = 22
EPS = 1e-6
NEGB = -30000.0
SPAN = 2048
ENGS = ["tensor", "vector", "scalar", "gpsimd", "sync"]

FULL_SEGS = [(0, 8192), (8192, 2048), (10240, 2048)]


class Sched:
    def __init__(self):
        self.phase = 0
        self._reset()

    def _reset(self):
        self.ops = {e: [] for e in ENGS}
        self.lastw = {}
        self.readers = {}
        if not hasattr(self, "cnt"):
            self.cnt = {e: 0 for e in ENGS}
            self.waited = {e: {} for e in ENGS}
            self.dmacnt = {}
            self.keys = []
            for e in ENGS:
                self._key(("E", e))

    def _key(self, k):
        if k not in self.keys:
            self.keys.append(k)
        return k

    def _collect(self, eng, reads, writes):
        deps = []
        for r in reads:
            ev = self.lastw.get(r)
            if ev is not None:
                deps.append((ev, True))
        for w in writes:
            ev = self.lastw.get(w)
            if ev is not None:
                deps.append((ev, False))
            for ev in self.readers.get(w, ()):
                deps.append((ev, False))
        waits = {}
        for (key, val, src), is_raw in deps:
            if src == eng:
                if eng == "tensor" or not is_raw:
                    continue
            if src == "dma":
                val = self.dmacnt[key]
            if self.waited[eng].get(key, 0) >= val:
                continue
            if waits.get(key, 0) < val:
                waits[key] = val
        for k, v in waits.items():
            self.waited[eng][k] = v
        return list(waits.items())

    def _record(self, ev, reads, writes):
        for r in reads:
            self.readers.setdefault(r, []).append(ev)
        for w in writes:
            self.lastw[w] = ev
            self.readers[w] = []

    def op(self, eng, fn, reads=(), writes=()):
        waits = self._collect(eng, reads, writes)
        self.cnt[eng] += 1
        key = ("E", eng)
        ev = (key, self.cnt[eng], eng)
        self.ops[eng].append((waits, fn, (key, 1)))
        self._record(ev, reads, writes)

    def dma(self, queue, name, fn, reads=(), writes=()):
        waits = self._collect(queue, reads, writes)
        key = self._key(("D", name))
        self.dmacnt[key] = self.dmacnt.get(key, 0) + 16
        ev = (key, self.dmacnt[key], "dma")
        self.ops[queue].append((waits, fn, (key, 16)))
        self._record(ev, reads, writes)

    def barrier(self):
        finals = [(("E", e), self.cnt[e]) for e in ENGS if self.cnt[e] > 0]
        finals += list(self.dmacnt.items())
        for e in ENGS:
            self.ops[e].append((finals, None, None))

    def replay(self, nc, sems):
        ops = self.ops
        with nc.Block() as block:
            def run(engname):
                def body(eng):
                    for waits, fn, inc in ops[engname]:
                        for k, v in waits:
                            eng.wait_ge(sems[k], v)
                        if fn is not None:
                            inst = fn(eng)
                            inst.then_inc(sems[inc[0]], inc[1])
                return body
            block.tensor(run("tensor"))
            block.vector(run("vector"))
            block.scalar(run("scalar"))
            block.gpsimd(run("gpsimd"))
            block.sync(run("sync"))

    def next_phase(self):
        self.phase += 1
        self._reset()


class PS:
    def __init__(self, banks):
        self.banks = banks
        self.groups = {}

    def setup(self, **groups):
        self.groups = {k: [list(v), 0] for k, v in groups.items()}

    def get(self, group):
        g = self.groups[group]
        b = g[0][g[1] % len(g[0])]
        g[1] += 1
        return b, self.banks[b], ("ps", b)


def t5_buckets(rel):
    nb = 16
    max_exact = 8
    n = np.abs(rel)
    large = max_exact + (np.log(np.maximum(n, 1) / max_exact)
                         / math.log(1024 / max_exact) * (nb - max_exact)).astype(np.int32)
    large = np.minimum(large, nb - 1)
    return ((rel > 0).astype(np.int32) * nb + np.where(n < max_exact, n, large)).astype(np.int32)


FAMS = [(1, 128), (1, 64), (4, 64), (16, 64)]


def static_consts():
    cst = np.zeros((128, 384), np.float32)
    cst[:, 0:128] = np.eye(128, dtype=np.float32)
    cst[0:64, 128:192] = 1.0
    cst[64:128, 192:256] = 1.0
    cst[0, 320:384] = 1.0
    oht = np.zeros((33, 4, 512), np.float32)
    rel = np.arange(512) - 255
    for f, (d, hw) in enumerate(FAMS):
        b = t5_buckets(d * rel)
        valid = (np.abs(rel) <= hw) & (np.arange(512) < 511)
        for n in range(512):
            if valid[n]:
                oht[b[n], f, n] = 1.0
            else:
                oht[32, f, n] = 1.0
    return cst, oht.reshape(33, 2048)


def build_program(segs, debug=False, nphase=6):
    T = sum(L for _, L in segs)
    spans = []
    for (sb, L) in segs:
        for s in range(L // SPAN):
            spans.append((sb + s * SPAN, sb, L))

    nc = bass.Bass("TRN2", target_bir_lowering=False)

    def din(name, shape, dt=F32):
        return nc.dram_tensor(name, list(shape), dt, kind="ExternalInput")

    x_in = din("x", [T, D])
    wqkvA = din("wqkvA", [128, 8, 1536])
    woA = din("woA", [128, 8, 1024])
    wqkvB = din("wqkvB", [128, 8, 4608])
    woB = din("woB", [128, 8, 1024])
    wgu = [din("wgu0", [128, 8, 2 * DFF]), din("wgu1", [128, 8, 2 * DFF])]
    wdn = [din("wd0", [128, NCH, D]), din("wd1", [128, NCH, D])]
    gn = din("gn", [4, 128, D])
    gqk_in = din("gqk", [128, 8])
    sink_in = din("sink", [1, 16])
    relt_in = din("relt", [32, 16])
    cst_in = din("cst", [128, 384])
    oht_in = din("oht", [33, 2048])
    okind = "ExternalOutput"
    y_out = nc.dram_tensor("y", [T, D], F32, kind=okind)
    ikind = okind if debug else "Internal"
    x1 = nc.dram_tensor("x1", [T, D], F32, kind=ikind)
    x2 = nc.dram_tensor("x2", [T, D], F32, kind=ikind)
    x3 = nc.dram_tensor("x3", [T, D], F32, kind=ikind)
    qA = nc.dram_tensor("qA", [1, 8, 128, T], BF16)
    kA = nc.dram_tensor("kA", [1, 2, 128, T], BF16)
    vA = nc.dram_tensor("vA", [1, T, 512], BF16)
    qB = nc.dram_tensor("qB", [3, 8, 128, T], BF16)
    kB = nc.dram_tensor("kB", [3, 2, 128, T], BF16)
    vB = nc.dram_tensor("vB", [3, T, 512], BF16)
    gv = nc.dram_tensor("gv", [4, 16, 512], F32)

    S = Sched()
    uid = [0]
    sems = {}

    with ExitStack() as gs:
        def galloc(name, shape, dt):
            return gs.enter_context(nc.sbuf_tensor("g_" + name, list(shape), dt))

        dbl = [gs.enter_context(nc.psum_tensor(f"psd{i}", [128, 1024], F32)) for i in range(4)]
        banks = []
        for i in range(4):
            banks.append(dbl[i][:, 0:512])
            banks.append(dbl[i][:, 512:1024])
        ps = PS(banks)
        ident = galloc("ident", [128, 128], BF16)
        blk = galloc("blk", [128, 128], BF16)
        sel = galloc("sel", [1, 128], BF16)
        esrow = galloc("esrow", [1, 2048], BF16)
        gqk = galloc("gqkt", [128, 4], F32)

        def finish_phase():
            S.barrier()
            for k in S.keys:
                if k not in sems:
                    nm = "s_" + "_".join(str(z) for z in k)
                    sems[k] = gs.enter_context(nc.semaphore(nm))
            with nc.allow_non_contiguous_dma(reason="strided scratch layouts"):
                S.replay(nc, sems)
            S.next_phase()

        with ExitStack() as pst:
            def A(name, shape, dt):
                uid[0] += 1
                return pst.enter_context(nc.sbuf_tensor(f"{name}_u{uid[0]}", list(shape), dt))
            ta = A("ta", [33, 16], F32)
            oht = A("oht", [33, 2048], F32)
            gvs = A("gvs", [16, 2048], F32)
            es = A("es", [1, 16], F32)
            es2 = A("es2", [1, 16], F32)
            gq = A("gq", [128, 8], F32)
            gtmp = A("gtmp", [128, 4], F32)

            S.dma("gpsimd", "c0", lambda e: e.dma_start(out=ident[:], in_=cst_in.ap()[:, 0:128]), writes=["ident"])
            S.dma("gpsimd", "c0", lambda e: e.dma_start(out=blk[:], in_=cst_in.ap()[:, 128:256]), writes=["blk"])
            S.dma("gpsimd", "c0", lambda e: e.dma_start(out=sel[:], in_=cst_in.ap()[0:1, 256:384]), writes=["sel"])
            S.dma("sync", "c1", lambda e: e.dma_start(out=ta[0:32, :], in_=relt_in.ap()), writes=["ta0"])
            S.op("vector", lambda e: e.memset(ta[32:33, :], NEGB), writes=["ta1"])
            S.dma("sync", "c1", lambda e: e.dma_start(out=oht[:], in_=oht_in.ap()), writes=["oht"])
            S.dma("sync", "c1", lambda e: e.dma_start(out=es[:], in_=sink_in.ap()), writes=["es"])
            S.dma("sync", "c1", lambda e: e.dma_start(out=gq[:], in_=gqk_in.ap()), writes=["gq"])
            for f in range(4):
                b, bank, bk = (f, banks[f], ("ps", f))
                S.op("tensor", lambda e, f=f, bank=bank: e.matmul(bank[0:16, :], ta[0:33, :], oht[0:33, f * 512:(f + 1) * 512],
                                                                  start=True, stop=True),
                     reads=["ta0", "ta1", "oht"], writes=[bk])
                S.op("vector", lambda e, f=f, bank=bank: e.tensor_copy(out=gvs[:, f * 512:(f + 1) * 512], in_=bank[0:16, :]),
                     reads=[bk], writes=[("gvs", f)])
                S.dma("sync", "c2", lambda e, f=f: e.dma_start(out=gv.ap()[f], in_=gvs[:, f * 512:(f + 1) * 512]),
                      reads=[("gvs", f)], writes=[("gv", f)])
            S.op("scalar", lambda e: e.activation(out=es2[:], in_=es[:], func=AF.Exp), reads=["es"], writes=["es2"])
            S.op("vector", lambda e: e.tensor_copy(out=esrow[:].rearrange("p (h q) -> p h q", q=128),
                                                   in_=bass.AP(es2, 0, [[16, 1], [1, 16], [0, 128]])),
                 reads=["es2"], writes=["esrow"])
            S.op("vector", lambda e: e.tensor_tensor(out=gtmp[:], in0=gq[:].rearrange("p (f t) -> p f t", t=2)[:, :, 0],
                                                     in1=gq[:].rearrange("p (f t) -> p f t", t=2)[:, :, 1], op=ALU.mult),
                 reads=["gq"], writes=["gtmp"])
            S.op("vector", lambda e: e.tensor_scalar(out=gqk[:], in0=gtmp[:], scalar1=0.125, scalar2=None, op0=ALU.mult),
                 reads=["gtmp"], writes=["gqk"])
            finish_phase()

        def phase_qkv(xsrc, wsrc, ngrp, dils, gidx, qdst, kdst, vdst, gqk_col0):
            with ExitStack() as pst:
                def A(name, shape, dt):
                    uid[0] += 1
                    return pst.enter_context(nc.sbuf_tensor(f"{name}_u{uid[0]}", list(shape), dt))
                WC = ngrp * 1536
                w = A("w", [128, 8, WC], BF16)
                gnt = A("gnt", [128, D], F32)
                xs = [A(f"xs{i}", [128, D], F32) for i in range(4)]
                junk = A("junk", [128, D], BF16)
                hb = [A(f"hb{i}", [128, D], BF16) for i in range(2)]
                hTs = [A(f"hT{i}", [128, 8, SPAN], BF16) for i in range(2)]
                ssq = A("ssq", [128, 16], F32)
                std = A("std", [128, 16], F32)
                rstd = A("rstd", [128, 16], F32)
                sq = [A(f"sq{i}", [128, 512], BF16) for i in range(3)]
                stdb = [A(f"stdb{i}", [128, 512], F32) for i in range(2)]
                rstdb = [A(f"rstdb{i}", [128, 512], F32) for i in range(2)]
                stage = [A(f"stage{i}", [128, SPAN], BF16) for i in range(2)]
                nvs = 2 if ngrp == 1 else 1
                vstage = [A(f"vstage{i}", [128, 16, 4, 128], BF16) for i in range(nvs)]
                ps.setup(tp=[0], mm=[1, 2, 3, 4], sb=[5, 6], v=[7, 1, 2, 3, 4])

                S.dma("sync", "gn", lambda e: e.dma_start(out=gnt[:], in_=gn.ap()[gidx]), writes=["gnt"])
                wloaded = set()

                def load_w(gi_, blk_):
                    if (gi_, blk_) in wloaded:
                        return
                    wloaded.add((gi_, blk_))
                    c0_, c1_ = gi_ * 1536 + blk_ * 512, gi_ * 1536 + (blk_ + 1) * 512
                    S.dma("gpsimd", "w0", lambda e, c0_=c0_, c1_=c1_: e.dma_start(out=w[:, :, c0_:c1_], in_=wsrc.ap()[:, :, c0_:c1_]),
                          writes=[("w", gi_, blk_)])
                for i in range(nvs):
                    S.op("gpsimd", lambda e, i=i: e.memset(vstage[i][:, :, :, 64:128], 1.0), writes=[("vst1", i)])
                sidx = [0]
                vidx = [0]
                ucnt = [0]

                qpend = [None]

                def q_stage1(u):
                    b, bank, bk = ps.get("mm")
                    u["bank"], u["bk"] = bank, bk
                    col0, mt = u["col0"], u["mt"]
                    load_w(u["gi"], (col0 % 1536) // 512)
                    hTc, par = u["hT"], u["par"]

                    def mm(e, bank=bank, col0=col0, mt=mt, hTc=hTc):
                        inst = None
                        for kc in range(8):
                            inst = e.matmul(bank[:, :], w[:, kc, col0:col0 + 128], hTc[:, kc, mt * 512:(mt + 1) * 512],
                                            start=(kc == 0), stop=(kc == 7))
                        return inst
                    S.op("tensor", mm, reads=[("w", u["gi"], (col0 % 1536) // 512), ("hT", par, mt)], writes=[bk])
                    uu = ucnt[0] % 3
                    ucnt[0] += 1
                    u["uu"] = uu
                    S.op("scalar", lambda e, bank=bank, uu=uu: e.activation(out=sq[uu][:], in_=bank[:, :], func=AF.Square),
                         reads=[bk], writes=[("sq", uu)])

                def q_stage2(u):
                    bank, bk, uu, d, mt, st, fc, gi = u["bank"], u["bk"], u["uu"], u["d"], u["mt"], u["st"], u["fc"], u["gi"]
                    v2 = uu % 2
                    b2, bank2, bk2 = ps.get("sb")
                    S.op("tensor", lambda e, bank2=bank2, uu=uu: e.matmul(bank2[:, :], blk[:], sq[uu][:], start=True, stop=True),
                         reads=[("sq", uu), "blk"], writes=[bk2])
                    S.op("scalar", lambda e, bank2=bank2, v2=v2: e.activation(out=stdb[v2][:], in_=bank2[:, :], func=AF.Ln,
                                                                             scale=1.0 / 64, bias=epsb[:, 0:1]),
                         reads=[bk2], writes=[("stdb", v2)])
                    S.op("scalar", lambda e, v2=v2: e.activation(out=rstdb[v2][:], in_=stdb[v2][:], func=AF.Exp, scale=-0.5),
                         reads=[("stdb", v2)], writes=[("rstdb", v2)])
                    nt = 512 // d
                    outv = stage[st][:, :].rearrange("p (r t) -> p r t", r=d)[:, :, mt * nt:(mt + 1) * nt]
                    if fc < 8:
                        S.op("vector", lambda e, bank=bank, v2=v2, outv=outv, d=d: e.tensor_tensor(
                            out=outv, in0=bank[:, :].rearrange("p (t r) -> p r t", r=d),
                            in1=rstdb[v2][:, :].rearrange("p (t r) -> p r t", r=d), op=ALU.mult),
                             reads=[bk, ("rstdb", v2)], writes=[("stage", st)])
                    else:
                        gc = gqk_col0 + gi
                        S.op("vector", lambda e, bank=bank, v2=v2, outv=outv, d=d, gc=gc: e.scalar_tensor_tensor(
                            out=outv, in0=bank[:, :].rearrange("p (t r) -> p r t", r=d), scalar=gqk[:, gc:gc + 1],
                            in1=rstdb[v2][:, :].rearrange("p (t r) -> p r t", r=d), op0=ALU.mult, op1=ALU.mult),
                             reads=[bk, ("rstdb", v2), "gqk"], writes=[("stage", st)])
                    if mt == 3:
                        dview = u["dview"]
                        S.dma("gpsimd", f"stg{st}", lambda e, st=st, dview=dview, d=d: e.dma_start(
                            out=dview, in_=stage[st][:, :].rearrange("p (r t) -> p r t", r=d)),
                              reads=[("stage", st)])

                def qpush(u):
                    q_stage1(u)
                    if qpend[0] is not None:
                        q_stage2(qpend[0])
                    qpend[0] = u

                def qflush():
                    if qpend[0] is not None:
                        q_stage2(qpend[0])
                        qpend[0] = None

                def a1(Pb, j):
                    sl = j % 4
                    if j == 0:
                        S.op("gpsimd", lambda e: e.memset(ssq[:], 0.0), writes=[("ssq", jj) for jj in range(16)])
                    S.dma("sync", f"xs{sl}", lambda e, sl=sl, j=j, Pb=Pb: e.dma_start(
                        out=xs[sl][:], in_=xsrc.ap()[Pb + j * 128:Pb + (j + 1) * 128, :]), writes=[("xs", sl)])
                    S.op("scalar", lambda e, sl=sl, j=j: e.activation(out=junk[:], in_=xs[sl][:], func=AF.Square,
                                                                     accum_out=ssq[:, j:j + 1]),
                         reads=[("xs", sl), ("ssq", j)], writes=["junk", ("ssq", j)])
                    S.op("scalar", lambda e, j=j: e.activation(out=std[:, j:j + 1], in_=ssq[:, j:j + 1], func=AF.Ln,
                                                              scale=1.0 / D, bias=epsb[:, 0:1]),
                         reads=[("ssq", j)], writes=[("std", j)])
                    S.op("scalar", lambda e, j=j: e.activation(out=rstd[:, j:j + 1], in_=std[:, j:j + 1], func=AF.Exp, scale=-0.5),
                         reads=[("std", j)], writes=[("rstd", j)])
                    hs = j % 2
                    S.op("vector", lambda e, j=j, sl=sl, hs=hs: e.scalar_tensor_tensor(
                        out=hb[hs][:], in0=xs[sl][:], scalar=rstd[:, j:j + 1], in1=gnt[:], op0=ALU.mult, op1=ALU.mult),
                         reads=[("xs", sl), ("rstd", j), "gnt"], writes=[("hb", hs)])

                def a2(par, j):
                    hs = j % 2
                    hTc = hTs[par]
                    b, bank, bk = ps.get("tp")
                    tpv = bank.bitcast(BF16)

                    def tr(e, hs=hs, tpv=tpv):
                        inst = None
                        for kc in range(8):
                            inst = e.transpose(tpv[:, kc * 128:(kc + 1) * 128], hb[hs][:, kc * 128:(kc + 1) * 128], ident[:])
                        return inst
                    S.op("tensor", tr, reads=[("hb", hs), "ident"], writes=[bk])
                    if j % 2 == 0:
                        S.op("scalar", lambda e, j=j, tpv=tpv, hTc=hTc: e.activation(
                            out=hTc[:, :, j * 128:(j + 1) * 128], in_=tpv[:, :].rearrange("p (k t) -> p k t", k=8), func=AF.Identity),
                             reads=[bk], writes=[("hT", par, j // 4)])
                    else:
                        S.op("vector", lambda e, j=j, tpv=tpv, hTc=hTc: e.tensor_copy(
                            out=hTc[:, :, j * 128:(j + 1) * 128], in_=tpv[:, :].rearrange("p (k t) -> p k t", k=8)),
                             reads=[bk], writes=[("hT", par, j // 4)])

                def a_steps(si_):
                    Pb_ = spans[si_][0]
                    par_ = si_ % 2
                    st_ = []
                    for j in range(16):
                        st_.append(lambda j=j: a1(Pb_, j))
                        if j >= 1:
                            st_.append(lambda j=j: a2(par_, j - 1))
                    st_.append(lambda: a2(par_, 15))
                    return st_

                asteps = []

                def a_emit(n):
                    for _ in range(n):
                        if asteps:
                            asteps.pop(0)()

                for f_ in a_steps(0):
                    f_()
                for si_, (Pb, Sb, L) in enumerate(spans):
                    par = si_ % 2
                    hT = hTs[par]
                    if si_ + 1 < len(spans):
                        asteps.extend(a_steps(si_ + 1))
                    T0s = Pb - Sb
                    for gi in range(ngrp):
                        d = dils[gi]
                        Wd_ = SPAN // d
                        Ls = L // d
                        T0 = T0s // d
                        for fc in range(10):
                            col0 = gi * 1536 + (fc * 128 if fc < 8 else 1024 + (fc - 8) * 128)
                            st = sidx[0] % 2
                            sidx[0] += 1
                            dst = (qdst.ap()[gi, fc] if fc < 8 else kdst.ap()[gi, fc - 8])
                            dview = dst[:, Sb:Sb + L].rearrange("p (r t) -> p r t", r=d)[:, :, T0:T0 + Wd_]
                            for mt in range(4):
                                qpush(dict(col0=col0, mt=mt, fc=fc, st=st, d=d, gi=gi, dview=dview, hT=hT, par=par))
                            a_emit(4)
                        qflush()
                        vs = vidx[0] % nvs
                        vidx[0] += 1
                        nq16 = 16 // d
                        for vt in range(16):
                            r = vt // nq16
                            qq = vt % nq16
                            c0 = qq * 128 * d + r
                            b, bank, bk = ps.get("v")
                            vcol = gi * 1536 + 1280

                            load_w(gi, 2)

                            def vmm(e, bank=bank, c0=c0, d=d, vcol=vcol, hTc=hT):
                                inst = None
                                for kc in range(8):
                                    lhsT = hTc[:, kc, c0:c0 + 127 * d + 1:d] if d > 1 else hTc[:, kc, c0:c0 + 128]
                                    inst = e.matmul(bank[:, 0:256], lhsT, w[:, kc, vcol:vcol + 256], start=(kc == 0), stop=(kc == 7))
                                return inst
                            S.op("tensor", vmm, reads=[("w", gi, 2)] + [("hT", par, i) for i in range(4)], writes=[bk])
                            if vt % 2 == 1:
                                a_emit(2)
                            S.op("vector", lambda e, bank=bank, vs=vs, vt=vt: e.tensor_copy(
                                out=vstage[vs][:, vt, :, 0:64], in_=bank[:, 0:256].rearrange("p (g c) -> p g c", g=4)),
                                 reads=[bk, ("vst1", vs)], writes=[("vstage", vs)])
                        vd = vdst.ap()[gi, Sb:Sb + L, :].rearrange("(r t) c -> r t c", r=d)[:, T0:T0 + Wd_, :]
                        vd = vd.rearrange("r (q p) c -> p r q c", p=128)
                        for r in range(d):
                            S.dma("gpsimd", f"vst{vs}", lambda e, vs=vs, vd=vd, r=r, nq16=nq16: e.dma_start(
                                out=vd[:, r], in_=vstage[vs][:, r * nq16:(r + 1) * nq16, :, :].rearrange("p q g c -> p q (g c)")),
                                  reads=[("vstage", vs)])
                    a_emit(len(asteps))
                finish_phase()

        def build_bt(A, BT, kinds, hk=None, pstride=2048, hkey="hk"):
            if hk is None:
                hk = A("hk", [128, 16, 128], F32)
            for k, (f, off) in enumerate(kinds):
                base = off + 128
                S.dma("sync", "hk", lambda e, f=f, base=base: e.dma_start(
                    out=bass.AP(hk, 0, [[pstride, 128], [128, 16], [1, 128]]),
                    in_=bass.AP(gv, f * 16 * 512 + base, [[1, 128], [512, 16], [1, 128]])), writes=[hkey])
                S.op("vector", lambda e, k=k: e.tensor_copy(out=BT[:, k, :, :], in_=bass.AP(hk, 127, [[pstride, 128], [128, 16], [-1, 128]])),
                     reads=[hkey], writes=["BT"])

        def phase_att_a(xsrc, xdst):
            with ExitStack() as pst:
                def A(name, shape, dt):
                    uid[0] += 1
                    return pst.enter_context(nc.sbuf_tensor(f"{name}_u{uid[0]}", list(shape), dt))
                wo = A("wo", [128, 8, D], BF16)
                BT = A("BT", [128, 3, 16, 128], BF16)
                qT = [A(f"qT{i}", [128, 8, SPAN], BF16) for i in range(2)]
                kT = [A(f"kT{i}", [128, 4, SPAN + 256], BF16) for i in range(2)]
                va = [A(f"va{i}", [128, 18, 512], BF16) for i in range(2)]
                xs = [A(f"xs{i}", [128, D], F32) for i in range(3)]
                pt = [A(f"pt{i}", [128, 512], BF16) for i in range(4)]
                rden = [A(f"rden{i}", [128, 512], F32) for i in range(2)]
                lnd = [A(f"lnd{i}", [128, 512], F32) for i in range(2)]
                oT = [A(f"oT{i}", [128, 8, 128], BF16) for i in range(2)]
                ps.setup(st=[0, 2, 4], ot=[6, 7], wo=[6, 7])
                for kc in range(8):
                    S.dma("gpsimd", "w", lambda e, kc=kc: e.dma_start(out=wo[:, kc, :], in_=woA.ap()[:, kc, :]), writes=[("wo", kc)])
                build_bt(A, BT, [(0, -128), (0, 0), (0, 128)])
                wokeys = [("wo", kc) for kc in range(8)]
                pti = [0]
                xi = [0]
                oi = [0]
                ri = [0]
                for si, (Pb, Sb, L) in enumerate(spans):
                    bf = si % 2
                    lo = max(Sb, Pb - 128)
                    hi = min(Sb + L, Pb + SPAN + 128)
                    c_lo = lo - (Pb - 128)
                    n = hi - lo
                    S.dma("sync", f"q{bf}", lambda e, bf=bf, Pb=Pb: e.dma_start(
                        out=qT[bf][:, 0:4, :], in_=qA.ap()[0, 0:4, :, Pb:Pb + SPAN].rearrange("c p t -> p c t")), writes=[("qT", bf)])
                    S.dma("sync", f"q{bf}", lambda e, bf=bf, Pb=Pb: e.dma_start(
                        out=qT[bf][:, 4:8, :], in_=qA.ap()[0, 4:8, :, Pb:Pb + SPAN].rearrange("c p t -> p c t")), writes=[("qT2", bf)])
                    for g in range(4):
                        for half in range(2):
                            S.dma("sync", f"k{bf}", lambda e, bf=bf, g=g, half=half, lo=lo, n=n, c_lo=c_lo: e.dma_start(
                                out=kT[bf][half * 64:(half + 1) * 64, g, c_lo:c_lo + n],
                                in_=kA.ap()[0, g // 2, (g % 2) * 64:(g % 2) * 64 + 64, lo:lo + n]), writes=[("kT", bf, g, half)])
                    kt_lo = c_lo // 128
                    nkt = n // 128
                    S.dma("sync", f"v{bf}", lambda e, bf=bf, lo=lo, n=n, kt_lo=kt_lo, nkt=nkt: e.dma_start(
                        out=va[bf][:, kt_lo:kt_lo + nkt, :],
                        in_=vA.ap()[0, lo:lo + n, :].rearrange("(k p) c -> p k c", p=128)), writes=[("va", bf)])
                    LV = 4
                    pend = []
                    deferred = []

                    def a_stage1(u, bf=bf):
                        g, kt, kind, j = u["g"], u["kt"], u["kind"], u["j"]
                        sb_, sbX, sbkX = ps.get("st")
                        sbY = banks[sb_ + 1]
                        sbkY = ("ps", sb_ + 1)

                        def smm(e, sbX=sbX, sbY=sbY, kind=kind, g=g, bf=bf, kt=kt, j=j):
                            btv = BT[:, kind, 4 * g:4 * g + 4, :].rearrange("p (c h) q -> p c h q", h=2)
                            e.matmul(sbX[:, 0:256], ident[:], btv[:, :, 0, :], start=True, stop=False)
                            e.matmul(sbY[:, 0:256], ident[:], btv[:, :, 1, :], start=True, stop=False)
                            inst = None
                            for hh in range(4):
                                half = hh % 2
                                c = hh // 2
                                ch = 2 * g + c
                                bank = sbX if half == 0 else sbY
                                inst = e.matmul(bank[:, c * 128:(c + 1) * 128],
                                                kT[bf][half * 64:(half + 1) * 64, g, kt * 128:(kt + 1) * 128],
                                                qT[bf][half * 64:(half + 1) * 64, ch, j * 128:(j + 1) * 128],
                                                start=False, stop=(c == 1))
                            return inst
                        S.op("tensor", smm, reads=["BT", "ident", ("qT", bf), ("qT2", bf), ("kT", bf, g, 0), ("kT", bf, g, 1)],
                             writes=[sbkX, sbkY])
                        p_ = pti[0] % 4
                        pti[0] += 1
                        u["p_"] = p_

                        def pexp(e, sb_=sb_, p_=p_):
                            ptv = pt[p_][:, :].rearrange("p (c h q) -> p c h q", c=2, h=2)
                            e.activation(out=ptv[:, :, 0, :], in_=banks[sb_][:, 0:256].rearrange("p (c q) -> p c q", c=2), func=AF.Exp)
                            return e.activation(out=ptv[:, :, 1, :], in_=banks[sb_ + 1][:, 0:256].rearrange("p (c q) -> p c q", c=2), func=AF.Exp)
                        S.op("scalar", pexp, reads=[sbkX, sbkY], writes=[("pt", p_)])

                    def a_stage2(u, bf=bf, Pb=Pb):
                        g, kt, j, ui, p_ = u["g"], u["kt"], u["j"], u["ui"], u["p_"]
                        obank, obk, osl, xsl = u["obank"], u["obk"], u["osl"], u["xsl"]
                        S.op("tensor", lambda e, obank=obank, bf=bf, kt=kt, g=g, p_=p_, ui=ui: e.matmul(
                            obank[:, :], va[bf][:, kt, g * 128:(g + 1) * 128], pt[p_][:], start=(ui == 0), stop=False),
                             reads=[("va", bf), ("pt", p_)], writes=[obk])
                        if not u["last"]:
                            return
                        S.op("tensor", lambda e, obank=obank, g=g: e.matmul(
                            obank[:, :], sel[0:1, :], esrow[0:1, g * 512:(g + 1) * 512], start=False, stop=True),
                             reads=["sel", "esrow"], writes=[obk])
                        r_ = ri[0] % 2
                        ri[0] += 1
                        S.op("scalar", lambda e, obank=obank, r_=r_: e.activation(out=lnd[r_][64:128, :], in_=obank[64:128, :], func=AF.Ln),
                             reads=[obk], writes=[("lnd", r_)])
                        S.op("scalar", lambda e, r_=r_: e.activation(out=rden[r_][64:128, :], in_=lnd[r_][64:128, :], func=AF.Exp, scale=-1.0),
                             reads=[("lnd", r_)], writes=[("rden", r_)])
                        for half in range(2):
                            S.op("vector", lambda e, obank=obank, r_=r_, half=half, g=g, osl=osl: e.tensor_tensor(
                                out=oT[osl][half * 64:(half + 1) * 64, 2 * g:2 * g + 2, :],
                                in0=obank[0:64, :].rearrange("p (c h q) -> p c h q", c=2, h=2)[:, :, half, :],
                                in1=rden[r_][64:128, :].rearrange("p (c h q) -> p c h q", c=2, h=2)[:, :, half, :],
                                op=ALU.mult),
                                 reads=[obk, ("rden", r_)], writes=[("oT", osl)])
                        if g != 3:
                            return

                        def wo_section(osl=osl, xsl=xsl, j=j):
                            sbw, _, _ = ps.get("st")
                            for nn in range(2):
                                wbank, wbk = banks[sbw + nn], ("ps", sbw + nn)

                                def womm(e, wbank=wbank, osl=osl, nn=nn):
                                    inst = None
                                    for c in range(8):
                                        inst = e.matmul(wbank[:, :], oT[osl][:, c, :], wo[:, c, nn * 512:(nn + 1) * 512],
                                                        start=(c == 0), stop=(c == 7))
                                    return inst
                                S.op("tensor", womm, reads=wokeys + [("oT", osl)], writes=[wbk])
                                S.op("vector", lambda e, wbank=wbank, xsl=xsl, nn=nn: e.tensor_tensor(
                                    out=xs[xsl][:, nn * 512:(nn + 1) * 512], in0=wbank[:, :], in1=xs[xsl][:, nn * 512:(nn + 1) * 512],
                                    op=ALU.add), reads=[wbk, ("xs", xsl)], writes=[("xs", xsl)])
                            S.dma("gpsimd", f"xo{xsl}", lambda e, xsl=xsl, j=j, Pb=Pb: e.dma_start(
                                out=xdst.ap()[Pb + j * 128:Pb + (j + 1) * 128, :], in_=xs[xsl][:]), reads=[("xs", xsl)])
                        deferred.append([2, wo_section])

                    def a_push(u):
                        a_stage1(u)
                        pend.append(u)
                        if len(pend) > 2:
                            a_stage2(pend.pop(0))
                        for dfr in list(deferred):
                            dfr[0] -= 1
                            if dfr[0] <= 0:
                                deferred.remove(dfr)
                                dfr[1]()

                    for j in range(16):
                        xsl = xi[0] % 3
                        xi[0] += 1
                        S.dma("sync", f"xs{xsl}", lambda e, xsl=xsl, j=j, Pb=Pb: e.dma_start(
                            out=xs[xsl][:], in_=xsrc.ap()[Pb + j * 128:Pb + (j + 1) * 128, :]), writes=[("xs", xsl)])
                        osl = oi[0] % 2
                        oi[0] += 1
                        for g in range(4):
                            units = []
                            for kind in range(3):
                                kt = j + kind
                                pos = Pb - 128 + kt * 128
                                if Sb <= pos < Sb + L:
                                    units.append((kt, kind))
                            ob, obank, obk = ps.get("ot")
                            for ui, (kt, kind) in enumerate(units):
                                a_push(dict(j=j, g=g, kt=kt, kind=kind, ui=ui, last=(ui == len(units) - 1),
                                            obank=obank, obk=obk, osl=osl, xsl=xsl))
                    while pend:
                        a_stage2(pend.pop(0))
                    for dfr in list(deferred):
                        dfr[1]()
                    deferred.clear()
                finish_phase()

        def phase_ffn(xsrc, xdst, wgu_d, wd_d, gidx):
            with ExitStack() as pst:
                def A(name, shape, dt):
                    uid[0] += 1
                    return pst.enter_context(nc.sbuf_tensor(f"{name}_u{uid[0]}", list(shape), dt))
                wg = A("wg", [128, 8, 2 * DFF], BF16)
                wd = A("wd", [128, NCH, D], BF16)
                gnt = A("gnt", [128, D], F32)
                xs = [A(f"xs{i}", [128, D], F32) for i in range(6)]
                junk = A("junk", [128, D], BF16)
                hb = [A(f"hb{i}", [128, D], BF16) for i in range(2)]
                hT = A("hT", [128, 8, 512], BF16)
                act = A("act", [128, NCH, 512], BF16)
                sg = [A(f"sg{i}", [128, 512], F32) for i in range(2)]
                ssq = A("ssq", [128, 4], F32)
                std = A("std", [128, 4], F32)
                rstd = A("rstd", [128, 4], F32)
                ps.setup(tp=[0], g=[1, 2], u=[3, 4], dn=[5, 6, 7])
                S.dma("sync", "gn", lambda e: e.dma_start(out=gnt[:], in_=gn.ap()[gidx]), writes=["gnt"])
                def load_wg(c):
                    for (nm, col) in (("wgg", c * 128), ("wgu", DFF + c * 128)):
                        S.dma("gpsimd", "w0", lambda e, col=col: e.dma_start(out=wg[:, :, col:col + 128], in_=wgu_d.ap()[:, :, col:col + 128]),
                              writes=[(nm, c)])

                def load_wd():
                    for c in range(NCH):
                        S.dma("gpsimd", "w0", lambda e, c=c: e.dma_start(out=wd[:, c, :], in_=wd_d.ap()[:, c, :]), writes=[("wd", c)])
                wdkeys = [("wd", c) for c in range(NCH)]
                nmt = T // 512
                si = [0]
                tpi = [0]

                def stage_a(m, j):
                    P0 = m * 512
                    sl = (4 * m + j) % 6
                    if j == 0:
                        S.op("gpsimd", lambda e: e.memset(ssq[:], 0.0), writes=[("ssq", jj) for jj in range(4)])
                    S.dma("sync", f"xs{sl}", lambda e, sl=sl, j=j, P0=P0: e.dma_start(
                        out=xs[sl][:], in_=xsrc.ap()[P0 + j * 128:P0 + (j + 1) * 128, :]), writes=[("xs", sl)])
                    S.op("scalar", lambda e, sl=sl, j=j: e.activation(out=junk[:], in_=xs[sl][:], func=AF.Square,
                                                                     accum_out=ssq[:, j:j + 1]),
                         reads=[("xs", sl), ("ssq", j)], writes=["junk", ("ssq", j)])
                    S.op("scalar", lambda e, j=j: e.activation(out=std[:, j:j + 1], in_=ssq[:, j:j + 1], func=AF.Ln,
                                                              scale=1.0 / D, bias=epsb[:, 0:1]),
                         reads=[("ssq", j)], writes=[("std", j)])
                    S.op("scalar", lambda e, j=j: e.activation(out=rstd[:, j:j + 1], in_=std[:, j:j + 1], func=AF.Exp, scale=-0.5),
                         reads=[("std", j)], writes=[("rstd", j)])
                    hs = tpi[0] % 2
                    tpi[0] += 1
                    S.op("vector", lambda e, j=j, sl=sl, hs=hs: e.scalar_tensor_tensor(
                        out=hb[hs][:], in0=xs[sl][:], scalar=rstd[:, j:j + 1], in1=gnt[:], op0=ALU.mult, op1=ALU.mult),
                         reads=[("xs", sl), ("rstd", j), "gnt"], writes=[("hb", hs)])
                    b, bank, bk = ps.get("tp")
                    tpv = bank.bitcast(BF16)

                    def tr(e, hs=hs, tpv=tpv):
                        inst = None
                        for kc in range(8):
                            inst = e.transpose(tpv[:, kc * 128:(kc + 1) * 128], hb[hs][:, kc * 128:(kc + 1) * 128], ident[:])
                        return inst
                    S.op("tensor", tr, reads=[("hb", hs), "ident"], writes=[bk])
                    S.op("vector", lambda e, j=j, tpv=tpv: e.tensor_copy(
                        out=hT[:, :, j * 128:(j + 1) * 128], in_=tpv[:, :].rearrange("p (k t) -> p k t", k=8)),
                         reads=[bk], writes=["hT"])

                for j in range(4):
                    stage_a(0, j)
                for m in range(nmt):
                    P0 = m * 512
                    for c in range(NCH):
                        if m == 0:
                            load_wg(c)
                            if c == NCH - 1:
                                load_wd()
                        gb, gbank, gbk = ps.get("g")
                        ub, ubank, ubk = ps.get("u")

                        def gmm(e, bank=gbank, col=c * 128):
                            inst = None
                            for kc in range(8):
                                inst = e.matmul(bank[:, :], wg[:, kc, col:col + 128], hT[:, kc, :], start=(kc == 0), stop=(kc == 7))
                            return inst
                        S.op("tensor", gmm, reads=[("wgg", c), "hT"], writes=[gbk])

                        def umm(e, bank=ubank, col=DFF + c * 128):
                            inst = None
                            for kc in range(8):
                                inst = e.matmul(bank[:, :], wg[:, kc, col:col + 128], hT[:, kc, :], start=(kc == 0), stop=(kc == 7))
                            return inst
                        S.op("tensor", umm, reads=[("wgu", c), "hT"], writes=[ubk])
                        s_ = si[0] % 2
                        si[0] += 1
                        S.op("scalar", lambda e, gbank=gbank, s_=s_: e.activation(out=sg[s_][:], in_=gbank[:, :], func=AF.Silu),
                             reads=[gbk], writes=[("sg", s_)])
                        S.op("vector", lambda e, ubank=ubank, s_=s_, c=c: e.tensor_tensor(
                            out=act[:, c, :], in0=ubank[:, :], in1=sg[s_][:], op=ALU.mult),
                             reads=[ubk, ("sg", s_)], writes=[("act", c)])
                    for j in range(4):
                        sl = (4 * m + j) % 6
                        for nn in range(2):
                            db, dbank, dbk = ps.get("dn")

                            def dmm(e, bank=dbank, j=j, nn=nn):
                                inst = None
                                for c in range(NCH):
                                    inst = e.matmul(bank[:, :], act[:, c, j * 128:(j + 1) * 128], wd[:, c, nn * 512:(nn + 1) * 512],
                                                    start=(c == 0), stop=(c == NCH - 1))
                                return inst
                            S.op("tensor", dmm, reads=wdkeys + [("act", c) for c in range(NCH)], writes=[dbk])
                            S.op("vector", lambda e, dbank=dbank, sl=sl, nn=nn: e.tensor_tensor(
                                out=xs[sl][:, nn * 512:(nn + 1) * 512], in0=dbank[:, :], in1=xs[sl][:, nn * 512:(nn + 1) * 512],
                                op=ALU.add), reads=[dbk, ("xs", sl)], writes=[("xs", sl)])
                        S.dma("gpsimd", f"xo{sl}", lambda e, sl=sl, j=j, P0=P0: e.dma_start(
                            out=xdst.ap()[P0 + j * 128:P0 + (j + 1) * 128, :], in_=xs[sl][:]), reads=[("xs", sl)])
                        if m + 1 < nmt and j >= 1:
                            stage_a(m + 1, j - 1)
                    if m + 1 < nmt:
                        stage_a(m + 1, 3)
                finish_phase()

        def phase_att_b(xsrc, xdst):
            dils = [1, 4, 16]
            with ExitStack() as pst:
                def A(name, shape, dt):
                    uid[0] += 1
                    return pst.enter_context(nc.sbuf_tensor(f"{name}_u{uid[0]}", list(shape), dt))
                wo = A("wo", [128, 8, D], BF16)
                BT = A("BT", [128, 6, 16, 128], BF16)
                acc = A("acc", [128, 4, SPAN], F32)
                rd = [A(f"rd{i}", [128, 512], F32) for i in range(3)]
                lnb = [A(f"lnb{i}", [128, 512], F32) for i in range(2)]
                oT = A("oT", [128, 8, SPAN], BF16)
                qT = [A(f"qT{i}", [128, 2, SPAN], BF16) for i in range(3)]
                kT = [A(f"kT{i}", [128, 4096], BF16) for i in range(3)]
                va = [A(f"va{i}", [128, 32, 128], BF16) for i in range(3)]
                xs = [A(f"xs{i}", [128, D], F32) for i in range(3)]
                pt = [A(f"pt{i}", [128, 512], BF16) for i in range(4)]
                ps.setup(st=[0, 2, 4], ot=[6, 7], wo=[6, 7])
                for kc in range(8):
                    S.dma("gpsimd", "w", lambda e, kc=kc: e.dma_start(out=wo[:, kc, :], in_=woB.ap()[:, kc, :]), writes=[("wo", kc)])
                build_bt(A, BT, [(1, -64), (1, 64), (2, -64), (2, 64), (3, -64), (3, 64)], hk=acc, pstride=4 * SPAN, hkey="acc")
                for i in range(3):
                    S.op("gpsimd", lambda e, i=i: e.memset(kT[i][:], 0.0), writes=[("kT", i)])
                wokeys = [("wo", kc) for kc in range(8)]
                pti = [0]
                xi = [0]
                li = [0]
                bpend = []
                npend = []

                def b_stage1(u):
                    gi, kk, g, bf, kc0, qc0 = u["gi"], u["kk"], u["g"], u["bf"], u["kc0"], u["qc0"]
                    sb_, sbX, sbkX = ps.get("st")
                    sbY = banks[sb_ + 1]
                    sbkY = ("ps", sb_ + 1)

                    def smm(e, sbX=sbX, sbY=sbY, gi=gi, kk=kk, g=g, bf=bf, kc0=kc0, qc0=qc0):
                        btv = BT[:, 2 * gi + kk, 4 * g:4 * g + 4, :].rearrange("p (c h) q -> p c h q", h=2)
                        e.matmul(sbX[:, 0:256], ident[:], btv[:, :, 0, :], start=True, stop=False)
                        e.matmul(sbY[:, 0:256], ident[:], btv[:, :, 1, :], start=True, stop=False)
                        inst = None
                        for hh in range(4):
                            half = hh % 2
                            c = hh // 2
                            bank = sbX if half == 0 else sbY
                            inst = e.matmul(bank[:, c * 128:(c + 1) * 128],
                                            kT[bf][half * 64:(half + 1) * 64, kc0:kc0 + 128],
                                            qT[bf][half * 64:(half + 1) * 64, c, qc0:qc0 + 128],
                                            start=False, stop=(c == 1))
                        return inst
                    S.op("tensor", smm, reads=["BT", "ident", ("qT", bf, 0), ("qT", bf, 1), ("kTl", bf, 0), ("kTl", bf, 1)],
                         writes=[sbkX, sbkY])
                    p_ = pti[0] % 4
                    pti[0] += 1
                    u["p_"] = p_

                    def pexp(e, sb_=sb_, p_=p_):
                        ptv = pt[p_][:, :].rearrange("p (c h q) -> p c h q", c=2, h=2)
                        e.activation(out=ptv[:, :, 0, :], in_=banks[sb_][:, 0:256].rearrange("p (c q) -> p c q", c=2), func=AF.Exp)
                        return e.activation(out=ptv[:, :, 1, :], in_=banks[sb_ + 1][:, 0:256].rearrange("p (c q) -> p c q", c=2), func=AF.Exp)
                    S.op("scalar", pexp, reads=[sbkX, sbkY], writes=[("pt", p_)])

                def b_stage2(u):
                    gi, kk, bf, ti, p_ = u["gi"], u["kk"], u["bf"], u["ti"], u["p_"]
                    obank, obk, accv = u["obank"], u["obk"], u["accv"]
                    S.op("tensor", lambda e, obank=obank, bf=bf, ti=ti, p_=p_, kk=kk: e.matmul(
                        obank[:, :], va[bf][:, ti, :], pt[p_][:], start=(kk == 0), stop=(kk == 1)),
                         reads=u["vkeys"] + [("pt", p_)], writes=[obk])
                    if kk == 0:
                        return
                    pk = u["pk"]
                    if gi == 0:
                        S.op("vector", lambda e, obank=obank, accv=accv: e.tensor_copy(
                            out=accv, in_=obank[:, :].rearrange("p (h q) -> p h q", h=4)),
                             reads=[obk], writes=["acc"] + pk)
                    else:
                        S.op("vector", lambda e, obank=obank, accv=accv: e.tensor_tensor(
                            out=accv, in0=obank[:, :].rearrange("p (h q) -> p h q", h=4), in1=accv, op=ALU.add),
                             reads=[obk] + pk, writes=["acc"] + pk)

                def b_push(u):
                    b_stage1(u)
                    bpend.append(u)
                    if len(bpend) > 2:
                        b_stage2(bpend.pop(0))

                def b_flush():
                    while bpend:
                        b_stage2(bpend.pop(0))

                for (Pb, Sb, L) in spans:
                    for g in range(4):
                        for gi in range(3):
                            d = dils[gi]
                            W_ = SPAN // d
                            Ls = L // d
                            T0 = (Pb - Sb) // d
                            nq = W_ // 128
                            KW = W_ + 128
                            bf = li[0] % 3
                            li[0] += 1
                            for c in range(2):
                                src = qB.ap()[gi, 2 * g + c, :, Sb:Sb + L].rearrange("p (r t) -> p r t", r=d)[:, :, T0:T0 + W_]
                                S.dma("sync", f"q{bf}", lambda e, bf=bf, c=c, src=src, d=d: e.dma_start(
                                    out=qT[bf][:, c, :].rearrange("p (r t) -> p r t", r=d), in_=src), writes=[("qT", bf, c)])
                            tlo = max(0, T0 - 64)
                            thi = min(Ls, T0 + W_ + 64)
                            koff = tlo - (T0 - 64)
                            kn = thi - tlo
                            for half in range(2):
                                src = kB.ap()[gi, g // 2, (g % 2) * 64:(g % 2) * 64 + 64, Sb:Sb + L].rearrange(
                                    "p (r t) -> p r t", r=d)[:, :, tlo:thi]
                                S.dma("sync", f"k{bf}", lambda e, bf=bf, half=half, src=src, d=d, KW=KW, koff=koff, kn=kn: e.dma_start(
                                    out=kT[bf][half * 64:(half + 1) * 64, 0:d * KW].rearrange("p (r t) -> p r t", r=d)[:, :, koff:koff + kn],
                                    in_=src), writes=[("kTl", bf, half)], reads=[("kT", bf)])
                            vkeys = []
                            for r in range(d):
                                m0 = 0
                                m1 = nq + 1
                                if T0 == 0:
                                    ti = r * (nq + 1)
                                    S.op("vector", lambda e, bf=bf, ti=ti: e.memset(va[bf][0:64, ti, :], 0.0), writes=[("va", bf, r, "z0")])
                                    row0 = Sb + r * Ls
                                    S.dma("sync", f"v{bf}", lambda e, bf=bf, ti=ti, row0=row0, gi=gi, g=g: e.dma_start(
                                        out=va[bf][64:128, ti, :], in_=vB.ap()[gi, row0:row0 + 64, g * 128:(g + 1) * 128]),
                                          writes=[("va", bf, r, "e0")])
                                    vkeys += [("va", bf, r, "z0"), ("va", bf, r, "e0")]
                                    m0 = 1
                                if T0 + W_ == Ls:
                                    ti = r * (nq + 1) + nq
                                    S.op("vector", lambda e, bf=bf, ti=ti: e.memset(va[bf][64:128, ti, :], 0.0), writes=[("va", bf, r, "z1")])
                                    row0 = Sb + r * Ls + Ls - 64
                                    S.dma("sync", f"v{bf}", lambda e, bf=bf, ti=ti, row0=row0, gi=gi, g=g: e.dma_start(
                                        out=va[bf][0:64, ti, :], in_=vB.ap()[gi, row0:row0 + 64, g * 128:(g + 1) * 128]),
                                          writes=[("va", bf, r, "e1")])
                                    vkeys += [("va", bf, r, "z1"), ("va", bf, r, "e1")]
                                    m1 = nq
                                if m1 > m0:
                                    row0 = Sb + r * Ls + T0 - 64 + m0 * 128
                                    nm = m1 - m0
                                    ti = r * (nq + 1) + m0
                                    S.dma("sync", f"v{bf}", lambda e, bf=bf, ti=ti, nm=nm, row0=row0, gi=gi, g=g: e.dma_start(
                                        out=va[bf][:, ti:ti + nm, :],
                                        in_=vB.ap()[gi, row0:row0 + nm * 128, g * 128:(g + 1) * 128].rearrange("(m p) c -> p m c", p=128)),
                                          writes=[("va", bf, r, "m")])
                                    vkeys.append(("va", bf, r, "m"))
                            for r in range(d):
                                for tq in range(nq):
                                    ob, obank, obk = ps.get("ot")
                                    accv = acc[:, :, :].rearrange("p h (t r) -> p h t r", r=d)[:, :, tq * 128:(tq + 1) * 128, r]
                                    if d == 1:
                                        while npend and npend[0][0] <= tq:
                                            npend.pop(0)[1]()
                                        pk = [("accp", tq // 4)]
                                    elif d == 4:
                                        pk = [("accp", tq)]
                                    else:
                                        pk = [("accp", i_) for i_ in range(4)]
                                    for kk in range(2):
                                        m = tq + kk
                                        b_push(dict(gi=gi, kk=kk, g=g, bf=bf, kc0=r * KW + m * 128, qc0=r * W_ + tq * 128,
                                                    ti=r * (nq + 1) + m, obank=obank, obk=obk, vkeys=vkeys, accv=accv, pk=pk))
                        b_flush()
                        def norm_piece(pc, g=g):
                            cs = slice(pc * 512, (pc + 1) * 512)
                            for hh in range(4):
                                half = hh % 2
                                nb_ = (pc * 4 + hh) % 2
                                if (pc * 4 + hh) % 5 == 2:
                                    nb_ = 2
                                    S.op("vector", lambda e, hh=hh, cs=cs: e.reciprocal(out=rd[2][0:64, :], in_=acc[64:128, hh, cs]),
                                         reads=[("accp", pc)], writes=[("rd", 2)])
                                else:
                                    S.op("scalar", lambda e, hh=hh, cs=cs, nb_=nb_: e.activation(out=lnb[nb_][64:128, :], in_=acc[64:128, hh, cs], func=AF.Ln),
                                         reads=[("accp", pc)], writes=[("lnb", nb_)])
                                    S.op("scalar", lambda e, nb_=nb_: e.activation(out=rd[nb_][0:64, :], in_=lnb[nb_][64:128, :], func=AF.Exp, scale=-1.0),
                                         reads=[("lnb", nb_)], writes=[("rd", nb_)])
                                S.op("vector", lambda e, hh=hh, half=half, g=g, cs=cs, nb_=nb_: e.tensor_tensor(
                                    out=oT[half * 64:(half + 1) * 64, 2 * g + hh // 2, cs], in0=acc[0:64, hh, cs], in1=rd[nb_][0:64, :], op=ALU.mult),
                                     reads=[("accp", pc), ("rd", nb_)], writes=[("oT", g)])
                        norm_piece(0)
                        if g < 3:
                            npend.extend([(2, lambda f=norm_piece: f(1)), (6, lambda f=norm_piece: f(2)), (10, lambda f=norm_piece: f(3))])
                        else:
                            for pc in range(1, 4):
                                norm_piece(pc)
                    for j in range(16):
                        xsl = xi[0] % 3
                        xi[0] += 1
                        S.dma("sync", f"xs{xsl}", lambda e, xsl=xsl, j=j, Pb=Pb: e.dma_start(
                            out=xs[xsl][:], in_=xsrc.ap()[Pb + j * 128:Pb + (j + 1) * 128, :]), writes=[("xs", xsl)])
                        for nn in range(2):
                            wb, wbank, wbk = ps.get("wo")

                            def womm(e, wbank=wbank, j=j, nn=nn):
                                inst = None
                                for c in range(8):
                                    inst = e.matmul(wbank[:, :], oT[:, c, j * 128:(j + 1) * 128], wo[:, c, nn * 512:(nn + 1) * 512],
                                                    start=(c == 0), stop=(c == 7))
                                return inst
                            S.op("tensor", womm, reads=wokeys + [("oT", g_) for g_ in range(4)], writes=[wbk])
                            S.op("vector", lambda e, wbank=wbank, xsl=xsl, nn=nn: e.tensor_tensor(
                                out=xs[xsl][:, nn * 512:(nn + 1) * 512], in0=wbank[:, :], in1=xs[xsl][:, nn * 512:(nn + 1) * 512],
                                op=ALU.add), reads=[wbk, ("xs", xsl)], writes=[("xs", xsl)])
                        S.dma("gpsimd", f"xo{xsl}", lambda e, xsl=xsl, j=j, Pb=Pb: e.dma_start(
                            out=xdst.ap()[Pb + j * 128:Pb + (j + 1) * 128, :], in_=xs[xsl][:]), reads=[("xs", xsl)])
                finish_phase()

        epsb = galloc("epsb", [128, 1], F32)
        S.op("vector", lambda e: e.memset(epsb[:], EPS), writes=["epsb"])
        finish_phase()

        if nphase >= 1:
            phase_qkv(x_in, wqkvA, 1, [1], 0, qA, kA, vA, 0)
        if nphase >= 2:
            phase_att_a(x_in, x1)
        if nphase >= 3:
            phase_ffn(x1, x2, wgu[0], wdn[0], 1)
        if nphase >= 4:
            phase_qkv(x2, wqkvB, 3, [1, 4, 16], 2, qB, kB, vB, 1)
        if nphase >= 5:
            phase_att_b(x2, x3)
        if nphase >= 6:
            phase_ffn(x3, y_out, wgu[1], wdn[1], 3)
    return nc


def _kc_layout(w):
    k, f = w.shape
    return np.ascontiguousarray(w.reshape(k // 128, 128, f).transpose(1, 0, 2))


def shared_inputs(rel_table, norm_attn, norm_ffn, a_w_qkv, a_q_gain, a_k_gain, a_sink, a_w_o,
                  b_w_qkv, b_q_gain, b_k_gain, b_w_o, ffn_w_gate_up, ffn_w_down):
    f = lambda a: np.asarray(a, dtype=np.float32)
    cst, oht = static_consts()
    gn = np.stack([np.broadcast_to(f(norm_attn)[0], (128, D)), np.broadcast_to(f(norm_ffn)[0], (128, D)),
                   np.broadcast_to(f(norm_attn)[1], (128, D)), np.broadcast_to(f(norm_ffn)[1], (128, D))]).copy()
    rep = lambda v: np.tile(f(v).reshape(64), 2)
    gqk = np.stack([rep(a_q_gain[0]), rep(a_k_gain[0]),
                    rep(b_q_gain[0][0]), rep(b_k_gain[0][0]),
                    rep(b_q_gain[0][1]), rep(b_k_gain[0][1]),
                    rep(b_q_gain[0][2]), rep(b_k_gain[0][2])], axis=1).copy()
    return {
        "wqkvA": _kc_layout(f(a_w_qkv)[0]), "woA": _kc_layout(f(a_w_o)[0]),
        "wqkvB": _kc_layout(f(b_w_qkv)[0]), "woB": _kc_layout(f(b_w_o)[0]),
        "wgu0": _kc_layout(f(ffn_w_gate_up)[0]), "wgu1": _kc_layout(f(ffn_w_gate_up)[1]),
        "wd0": _kc_layout(f(ffn_w_down)[0]), "wd1": _kc_layout(f(ffn_w_down)[1]),
        "gn": gn, "gqk": gqk, "sink": f(a_sink).reshape(1, 16), "relt": f(rel_table),
        "cst": cst, "oht": oht,
    }


_PROGRAM = {}


def kernel(x_prompt, x_sample, rel_table, norm_attn, norm_ffn, a_w_qkv, a_q_gain, a_k_gain,
           a_sink, a_w_o, b_w_qkv, b_q_gain, b_k_gain, b_w_o, ffn_w_gate_up, ffn_w_down):
    x_prompt = np.asarray(x_prompt, dtype=np.float32)
    x_sample = np.asarray(x_sample, dtype=np.float32)
    shared = shared_inputs(rel_table, norm_attn, norm_ffn, a_w_qkv, a_q_gain, a_k_gain, a_sink, a_w_o,
                           b_w_qkv, b_q_gain, b_k_gain, b_w_o, ffn_w_gate_up, ffn_w_down)
    n = 8
    in_maps = []
    for c in range(n):
        xc = np.concatenate([x_sample[c], x_prompt[2 * c], x_prompt[2 * c + 1]], axis=0)
        m = dict(shared)
        m["x"] = np.ascontiguousarray(xc)
        in_maps.append(m)
    if "full" not in _PROGRAM:
        _PROGRAM["full"] = build_program(FULL_SEGS)
    nc = _PROGRAM["full"]
    res = run_bass_kernel_spmd(nc, in_maps, core_ids=list(range(n)))
    y_prompt = np.empty_like(x_prompt)
    y_sample = np.empty_like(x_sample)
    for c in range(n):
        y = np.asarray(res.results[c]["y"], dtype=np.float32)
        y_sample[c] = y[0:8192]
        y_prompt[2 * c] = y[8192:10240]
        y_prompt[2 * c + 1] = y[10240:12288]
    return (y_prompt, y_sample)
```
